# Optimizing a Trainium2 kernel written in Bass

```python
import math
import jax
import jax.numpy as jnp
from jax import lax
import numpy as np

D_MODEL = 1024
BATCH = 8
SEQ = 2048
DEPTH = 2
DEC_BATCH = 128
DEC_SEQ = 1
PAST_LEN = 16384
PAGE_SIZE = 128

N_A_LAYERS = DEPTH // 2
N_B_LAYERS = DEPTH - N_A_LAYERS
A_HEADS = 8
A_DK = D_MODEL // A_HEADS
A_DV = D_MODEL // A_HEADS
A_CHUNK = 64
B_HEAD_DIM = 64
B_HEADS = D_MODEL // B_HEAD_DIM
B_KV_HEADS = 4
B_GROUP = B_HEADS // B_KV_HEADS
WINDOW = 128
BLOCK = 128
ATTN_SCALE = 1.0 / math.sqrt(B_HEAD_DIM)
N_BUCKETS = 32
MAX_DISTANCE = 128
D_FF = 4 * D_MODEL
EPS = 1e-6
NEG = -1e30

kernel_name = 'yoco_hgrn2_swa_sink_decoder'


def rmsnorm(x, g):
    xf = x.astype(jnp.float32)
    y = xf * lax.rsqrt(jnp.mean(xf * xf, axis=-1, keepdims=True) + EPS)
    return (y * g.astype(jnp.float32)).astype(x.dtype)


def sq_relu_mlp(h, w_up, w_down):
    u = jnp.maximum(h @ w_up, 0)
    return (u * u) @ w_down


def t5_bucket(dist):
    n = jnp.maximum(dist, 0)
    max_exact = N_BUCKETS // 2
    nf = jnp.maximum(n, 1).astype(jnp.float32)
    large = max_exact + (jnp.log(nf / max_exact) / math.log(MAX_DISTANCE / max_exact)
                         * (N_BUCKETS - max_exact)).astype(jnp.int32)
    large = jnp.minimum(large, N_BUCKETS - 1)
    return jnp.where(n < max_exact, n, large)


def t5_bias(dist, rel_bias):
    b = rel_bias[t5_bucket(dist)].astype(jnp.float32)
    q_len, k_len = dist.shape
    return jnp.transpose(b, (2, 0, 1)).reshape(B_KV_HEADS, B_GROUP, q_len, k_len)


def sink_attend(s, sink, v, eq):
    sink = sink.astype(jnp.float32)
    m = jnp.maximum(jnp.max(s, axis=-1, keepdims=True), sink)
    p = jnp.exp(s - m)
    denom = jnp.sum(p, axis=-1, keepdims=True) + jnp.exp(sink - m)
    return jnp.einsum(eq, (p / denom).astype(v.dtype), v)


def swa_prompt(q, k, v, sink, rel_bias):
    bsz, seq_len = q.shape[:2]
    nb = seq_len // BLOCK
    qb = q.reshape(bsz, nb, BLOCK, B_KV_HEADS, B_GROUP, B_HEAD_DIM)
    kb = k.reshape(bsz, nb, BLOCK, B_KV_HEADS, B_HEAD_DIM)
    vb = v.reshape(bsz, nb, BLOCK, B_KV_HEADS, B_HEAD_DIM)
    pad = ((0, 0), (1, 0), (0, 0), (0, 0), (0, 0))
    kk = jnp.concatenate([jnp.pad(kb, pad)[:, :-1], kb], axis=2)
    vv = jnp.concatenate([jnp.pad(vb, pad)[:, :-1], vb], axis=2)
    s = jnp.einsum('bnqhgd,bnkhd->bnhgqk', qb, kk).astype(jnp.float32) * ATTN_SCALE
    qi = jnp.arange(BLOCK)[:, None]
    kj = jnp.arange(2 * BLOCK)[None, :]
    dist = qi + BLOCK - kj
    valid = (dist >= 0) & (dist < WINDOW)
    valid = valid[None] & ((jnp.arange(nb)[:, None, None] > 0) | (kj >= BLOCK)[None])
    s = jnp.where(valid[None, :, None, None], s + t5_bias(dist, rel_bias), NEG)
    o = sink_attend(s, sink.reshape(B_KV_HEADS, B_GROUP)[:, :, None, None], vv,
                    'bnhgqk,bnkhd->bnqhgd')
    return o.reshape(bsz, seq_len, B_HEADS * B_HEAD_DIM)


def swa_sample(q, kk, vv, n_buf, sink, rel_bias):
    bsz, new_len = q.shape[:2]
    s = jnp.einsum('bqhgd,bkhd->bhgqk', q, kk).astype(jnp.float32) * ATTN_SCALE
    dist = (jnp.arange(new_len)[:, None] + n_buf) - jnp.arange(n_buf + new_len)[None, :]
    valid = (dist >= 0) & (dist < WINDOW)
    s = jnp.where(valid, s + t5_bias(dist, rel_bias), NEG)
    o = sink_attend(s, sink.reshape(B_KV_HEADS, B_GROUP)[:, :, None, None], vv,
                    'bhgqk,bkhd->bqhgd')
    return o.reshape(bsz, new_len, B_HEADS * B_HEAD_DIM)


def hgrn2_chunked(q, k, v, log_f, s0):
    bsz, seq_len = q.shape[:2]
    c = A_CHUNK
    n = seq_len // c

    def to_chunks(t):
        return t.reshape(bsz, n, c, t.shape[2], t.shape[3]).transpose(1, 0, 3, 2, 4)

    qc, kc, vc, fc = to_chunks(q), to_chunks(k), to_chunks(v), to_chunks(log_f)
    b = jnp.cumsum(fc, axis=3)
    b_ref = b[:, :, :, c // 2 - 1:c // 2]
    att = jnp.einsum('nbhtk,nbhsk->nbhts', qc * jnp.exp(b - b_ref), kc * jnp.exp(b_ref - b))
    att = jnp.where(jnp.tril(jnp.ones((c, c), dtype=bool)), att, 0.0)
    o_intra = jnp.einsum('nbhts,nbhsv->nbhtv', att, vc)
    b_last = b[:, :, :, -1:]
    q_inter = qc * jnp.exp(b)
    k_state = kc * jnp.exp(b_last - b)
    decay = jnp.exp(b_last[:, :, :, 0])

    def step(state, xs):
        qi, ks, vs, dl = xs
        o = jnp.einsum('bhtk,bhkv->bhtv', qi, state)
        state = dl[..., None] * state + jnp.einsum('bhsk,bhsv->bhkv', ks, vs)
        return state, o

    s_final, o_inter = lax.scan(step, s0, (q_inter, k_state, vc, decay))
    o = (o_intra + o_inter).transpose(1, 0, 3, 2, 4).reshape(bsz, seq_len, A_HEADS, A_DV)
    return o, s_final


def hgrn2_recurrent(q, k, v, log_f, s0):
    def step(state, xs):
        qt, kt, vt, ft = xs
        state = jnp.exp(ft)[..., None] * state + kt[..., :, None] * vt[..., None, :]
        return state, jnp.einsum('bhk,bhkv->bhv', qt, state)

    xs = tuple(jnp.swapaxes(t, 0, 1) for t in (q, k, v, log_f))
    s_final, o = lax.scan(step, s0, xs)
    return jnp.swapaxes(o, 0, 1), s_final


def hgrn2_mixer(h, s0, w_in, lb, g_norm, w_out, per_token):
    bsz, seq_len, _ = h.shape
    hk = A_HEADS * A_DK
    hv = A_HEADS * A_DV
    q, f, i, g = jnp.split(h @ w_in, [hk, 2 * hk, 2 * hk + hv], axis=-1)
    f32 = jnp.float32
    forget = lb + (1.0 - lb) * jax.nn.sigmoid(f.astype(f32))
    log_f = jnp.log(forget).reshape(bsz, seq_len, A_HEADS, A_DK)
    k = (1.0 - forget).reshape(bsz, seq_len, A_HEADS, A_DK)
    q = q.astype(f32).reshape(bsz, seq_len, A_HEADS, A_DK)
    v = i.astype(f32).reshape(bsz, seq_len, A_HEADS, A_DV)
    s0 = s0.astype(f32)
    if per_token:
        o, s_new = hgrn2_recurrent(q, k, v, log_f, s0)
    else:
        o, s_new = hgrn2_chunked(q, k, v, log_f, s0)
    o = rmsnorm(o, g_norm).reshape(bsz, seq_len, hv) * jax.nn.silu(g.astype(f32))
    return o.astype(h.dtype) @ w_out, s_new


def shared_kv(x, g_kv, w_kv):
    bsz, seq_len = x.shape[:2]
    k, v = jnp.split(rmsnorm(x, g_kv) @ w_kv, 2, axis=-1)
    shp = (bsz, seq_len, B_KV_HEADS, B_HEAD_DIM)
    return k.reshape(shp), v.reshape(shp)


def trunk(x, s_in, k_buf, v_buf, w_a_in, a_lb, a_gnorm, w_a_out, g_mix, g_mlp, g_kv, w_kv,
          w_b_q, b_sink, w_b_out, rel_bias, w_up, w_down, g_final, prompt):
    bsz, seq_len, _ = x.shape
    lb_all = jnp.cumsum(jax.nn.softmax(a_lb.astype(jnp.float32), axis=0), axis=0)
    a_states = []
    kk = vv = new_k = new_v = None
    n_buf = 0
    for layer in range(DEPTH):
        if layer < N_A_LAYERS:
            h = rmsnorm(x, g_mix[layer])
            if prompt:
                s0 = jnp.zeros((bsz, A_HEADS, A_DK, A_DV), jnp.float32)
            else:
                s0 = s_in[layer]
            o, s_new = hgrn2_mixer(h, s0, w_a_in[layer], lb_all[layer], a_gnorm[layer],
                                   w_a_out[layer], per_token=not prompt)
            a_states.append(s_new.astype(x.dtype))
            x = x + o
        else:
            j = layer - N_A_LAYERS
            if j == 0:
                k, v = shared_kv(x, g_kv, w_kv)
                if prompt:
                    kk, vv = k, v
                    w_keep = min(WINDOW, seq_len)
                    new_k, new_v = k[:, seq_len - w_keep:], v[:, seq_len - w_keep:]
                else:
                    n_buf = k_buf.shape[1]
                    kk = jnp.concatenate([k_buf.astype(k.dtype), k], axis=1)
                    vv = jnp.concatenate([v_buf.astype(v.dtype), v], axis=1)
                    new_k, new_v = kk[:, -n_buf:], vv[:, -n_buf:]
            h = rmsnorm(x, g_mix[layer])
            q = (h @ w_b_q[j]).reshape(bsz, seq_len, B_KV_HEADS, B_GROUP, B_HEAD_DIM)
            if prompt:
                a = swa_prompt(q, kk, vv, b_sink[j], rel_bias)
            else:
                a = swa_sample(q, kk, vv, n_buf, b_sink[j], rel_bias)
            x = x + a @ w_b_out[j]
        x = x + sq_relu_mlp(rmsnorm(x, g_mlp[layer]), w_up[layer], w_down[layer])
    return rmsnorm(x, g_final), jnp.stack(a_states), new_k, new_v


def setup_inputs(seed: int = 0) -> dict:
    key = jax.random.key(seed)
    ks = jax.random.split(key, 24)
    f32 = jnp.float32
    d = D_MODEL
    w_buf = min(WINDOW, PAST_LEN)

    def nrm(k, shape, scale):
        return jax.random.normal(k, shape, f32) * scale

    a_in_cols = 2 * A_HEADS * A_DK + 2 * A_HEADS * A_DV
    return {
        'x_prompt': nrm(ks[0], (BATCH, SEQ, d), 1.0),
        'x_sample': nrm(ks[1], (DEC_BATCH, DEC_SEQ, d), 1.0),
        'state_hgrn': nrm(ks[2], (N_A_LAYERS, DEC_BATCH, A_HEADS, A_DK, A_DV), 0.3),
        'cache_k_win': nrm(ks[3], (DEC_BATCH, w_buf, B_KV_HEADS, B_HEAD_DIM), 1.0),
        'cache_v_win': nrm(ks[4], (DEC_BATCH, w_buf, B_KV_HEADS, B_HEAD_DIM), 1.0),
        'w_a_in': nrm(ks[5], (N_A_LAYERS, d, a_in_cols), d ** -0.5),
        'a_lb': nrm(ks[6], (N_A_LAYERS + 1, A_HEADS * A_DK), 0.5),
        'a_gnorm': 1.0 + nrm(ks[7], (N_A_LAYERS, A_DV), 0.02),
        'w_a_out': nrm(ks[8], (N_A_LAYERS, A_HEADS * A_DV, d), (A_HEADS * A_DV) ** -0.5),
        'g_mix': 1.0 + nrm(ks[9], (DEPTH, d), 0.02),
        'g_mlp': 1.0 + nrm(ks[10], (DEPTH, d), 0.02),
        'g_kv': 1.0 + nrm(ks[11], (d,), 0.02),
        'w_kv': nrm(ks[12], (d, 2 * B_KV_HEADS * B_HEAD_DIM), d ** -0.5),
        'w_b_q': nrm(ks[13], (N_B_LAYERS, d, B_HEADS * B_HEAD_DIM), d ** -0.5),
        'b_sink': nrm(ks[14], (N_B_LAYERS, B_HEADS), 0.5),
        'w_b_out': nrm(ks[15], (N_B_LAYERS, B_HEADS * B_HEAD_DIM, d), (B_HEADS * B_HEAD_DIM) ** -0.5),
        'rel_bias': nrm(ks[16], (N_BUCKETS, B_HEADS), 0.5),
        'w_up': nrm(ks[17], (DEPTH, d, D_FF), d ** -0.5),
        'w_down': nrm(ks[18], (DEPTH, D_FF, d), D_FF ** -0.5),
        'g_final': 1.0 + nrm(ks[19], (d,), 0.02),
    }


def reference(x_prompt, x_sample, state_hgrn, cache_k_win, cache_v_win, w_a_in, a_lb, a_gnorm,
              w_a_out, g_mix, g_mlp, g_kv, w_kv, w_b_q, b_sink, w_b_out, rel_bias, w_up, w_down,
              g_final):
    y_prompt, s_prompt, k_prompt, v_prompt = trunk(
        x_prompt, None, None, None, w_a_in, a_lb, a_gnorm, w_a_out, g_mix, g_mlp, g_kv, w_kv,
        w_b_q, b_sink, w_b_out, rel_bias, w_up, w_down, g_final, prompt=True)
    y_sample, s_sample, k_sample, v_sample = trunk(
        x_sample, state_hgrn, cache_k_win, cache_v_win, w_a_in, a_lb, a_gnorm, w_a_out, g_mix,
        g_mlp, g_kv, w_kv, w_b_q, b_sink, w_b_out, rel_bias, w_up, w_down, g_final, prompt=False)
    return (y_prompt, y_sample, s_prompt, s_sample, k_prompt, v_prompt, k_sample, v_sample)
```

```python
from contextlib import ExitStack
import math
import os
import numpy as np
import concourse.bass as bass
import concourse.mybir as mybir
from concourse.bass_utils import run_bass_kernel_spmd

F32 = mybir.dt.float32
BF16 = mybir.dt.bfloat16
AF = mybir.ActivationFunctionType
ALU = mybir.AluOpType

NCORES = 8
SEQ = 2048
NS = 16
T = SEQ + NS
TILES = [(0, 512), (512, 512), (1024, 512), (1536, 512), (2048, NS)]
EPS = 1e-6
NEG = -30000.0
ATTN_SCALE = 0.125
SKIP = set(os.environ.get('DEVSKIP', '').split(','))
SAME_ENG_FREE = os.environ.get('SAME_ENG_FREE', '0') == '1'

ENGS = ["pe", "act", "dve", "pool", "sp"]


class Gran:
    __slots__ = ("w", "r")

    def __init__(self):
        self.w = None
        self.r = []


class Op:
    __slots__ = ("eng", "fn", "deps", "dma", "sig", "val", "sem", "pos", "prev")


class Prog:
    def __init__(self):
        self.ops = []
        self.by_eng = {e: [] for e in ENGS}
        self.out_dmas = []

    def record(self):
        self.rec = []

    def stop(self):
        r, self.rec = self.rec, None
        return r

    def replay(self, *lists):
        idx = [0] * len(lists)
        while True:
            best, bf = None, None
            for li, l in enumerate(lists):
                if idx[li] < len(l):
                    f = idx[li] / float(len(l))
                    if best is None or f < bf:
                        best, bf = li, f
            if best is None:
                break
            self.add(*lists[best][idx[best]])
            idx[best] += 1

    def add(self, eng, fn, reads=(), writes=(), dma=False, out=False):
        if getattr(self, "rec", None) is not None:
            self.rec.append((eng, fn, list(reads), list(writes), dma, out))
            return None
        op = Op()
        op.eng, op.fn, op.dma, op.sig, op.val, op.sem, op.prev = eng, fn, dma, False, 0, None, None
        op.pos = len(self.ops)
        deps = set()
        for g in reads:
            if g.w is not None:
                deps.add(g.w)
        for g in writes:
            if g.w is not None:
                deps.add(g.w)
            deps.update(g.r)
        for g in writes:
            g.w = op
            g.r = []
        for g in reads:
            if g.w is not op:
                g.r.append(op)
        deps.discard(op)
        best = {}
        keep = set()
        for d in deps:
            if d.dma:
                keep.add(d)
            else:
                if d.eng == eng and not dma and (eng == "pe" or SAME_ENG_FREE):
                    continue
                b = best.get(d.eng)
                if b is None or d.pos > b.pos:
                    best[d.eng] = d
        keep.update(best.values())
        op.deps = keep
        for d in keep:
            d.sig = True
        self.ops.append(op)
        self.by_eng[eng].append(op)
        if out:
            self.out_dmas.append(op)
        return op

    def barrier(self, engs=("pe", "act", "dve", "pool")):
        lasts = []
        for e in engs:
            for o in reversed(self.by_eng[e]):
                if o.fn is not None and not o.dma:
                    lasts.append(o)
                    break
        start = getattr(self, "bar_pos", 0)
        dmas = [o for o in self.ops[start:] if o.dma and o.eng in ("sp", "act")]
        self.bar_pos = len(self.ops)
        for e in engs:
            op = Op()
            op.eng, op.fn, op.dma, op.sig, op.val, op.sem, op.prev = e, None, False, False, 0, None, None
            op.pos = len(self.ops)
            op.deps = set(l for l in lasts if l.eng != e) | set(dmas)
            for d in op.deps:
                d.sig = True
            self.ops.append(op)
            self.by_eng[e].append(op)

    def emit(self, nc, block, ES):
        ndma = {"sp": 24, "pool": 12, "act": 4}
        sems = []

        def newsem(name):
            s = ES.enter_context(nc.semaphore(name))
            sems.append(s)
            return len(sems) - 1

        eng_sem = {e: newsem("s_" + e) for e in ENGS}
        dma_sem = {q: [newsem("d_%s%d" % (q, i)) for i in range(n)] for q, n in ndma.items()}
        dma_cnt = {q: 0 for q in ndma}
        dma_last = {q: [None] * n for q, n in ndma.items()}
        dma_use = {q: [0] * n for q, n in ndma.items()}
        cnt = {e: 0 for e in ENGS}
        for op in self.ops:
            if op.dma:
                q = op.eng
                slot = dma_cnt[q] % ndma[q]
                dma_cnt[q] += 1
                op.sem = dma_sem[q][slot]
                dma_use[q][slot] += 1
                op.val = 16 * dma_use[q][slot]
                op.prev = dma_last[q][slot]
                dma_last[q][slot] = op
            elif op.sig:
                cnt[op.eng] += 1
                op.val = cnt[op.eng]
                op.sem = eng_sem[op.eng]
        handles = {"pe": "tensor", "act": "scalar", "dve": "vector", "pool": "gpsimd", "sp": "sync"}

        def run(ename, eng):
            seen = {}
            ops = self.by_eng[ename]
            for op in ops:
                waits = {}
                ds = list(op.deps)
                if op.prev is not None:
                    ds.append(op.prev)
                for d in ds:
                    if waits.get(d.sem, 0) < d.val:
                        waits[d.sem] = d.val
                for s, v in sorted(waits.items()):
                    if seen.get(s, 0) < v:
                        eng.wait_ge(sems[s], v)
                        seen[s] = v
                if op.fn is None:
                    continue
                ins = op.fn(eng)
                if op.dma:
                    ins.then_inc(sems[op.sem], 16)
                elif op.sig:
                    ins.then_inc(sems[op.sem], 1)
            if ename == "sp":
                for op in self.out_dmas:
                    if seen.get(op.sem, 0) < op.val:
                        eng.wait_ge(sems[op.sem], op.val)
                        seen[op.sem] = op.val

        for ename in ENGS:
            getattr(block, handles[ename])(lambda eng, ename=ename: run(ename, eng))


def _t5_bucket(dist):
    n = np.maximum(dist, 0)
    nf = np.maximum(n, 1).astype(np.float32)
    large = 16 + (np.log(nf / np.float32(16)) / np.float32(math.log(128 / 16)) * np.float32(16)).astype(np.int32)
    large = np.minimum(large, 31)
    return np.where(n < 16, n, large)


NCF = 128 + 128 + 512 + 2 + 128


def make_consts():
    cf = np.zeros((128, NCF), np.float32)
    cf[:, 0:128] = np.eye(128, dtype=np.float32)
    s = np.arange(128)[:, None]
    t = np.arange(128)[None, :]
    cf[:, 128:256] = ((s // 64 == t // 64) & (t >= s)).astype(np.float32)
    cf[:, 256:768] = (np.arange(512) % 64 != 0).astype(np.float32)[None, :]
    cf[:64, 768] = 1.0
    cf[64:, 769] = 1.0
    cf[:, 770:898] = np.eye(128, dtype=np.float32)[::-1]
    oh = np.zeros((32, 512), np.float32)
    mrow = np.full((16, 512), NEG, np.float32)
    for kb in range(2):
        for delta in range(-127, 128):
            dist = delta if kb == 1 else delta + 128
            if 0 <= dist < 128:
                b = int(_t5_bucket(np.array([dist]))[0])
                oh[b, kb * 256 + delta + 127] = 1.0
                mrow[:, kb * 256 + delta + 127] = 0.0
    t5 = np.zeros((32, 1024), np.float32)
    t5[:, 0:512] = oh
    t5[0:16, 512:1024] = mrow
    return cf, t5


def build(flags=None):
    fl = dict(hgrn=True, mlp0=True, kv=True, attn=True, mlp1=True, sample=True)
    if flags:
        fl.update(flags)
    nc = bass.Bass("TRN2", target_bir_lowering=False)
    P = Prog()
    ES = ExitStack()

    def din(name, shape):
        return nc.dram_tensor(name, shape, F32, kind="ExternalInput").ap()

    def dout(name, shape):
        return nc.dram_tensor(name, shape, F32, kind="ExternalOutput").ap()

    xp = din("xp", [SEQ, 1024])
    xs = din("xs", [NS, 1024])
    st_in = din("st", [NS, 8, 128, 128])
    ck = din("ck", [NS, 128, 256])
    cv = din("cv", [NS, 128, 256])
    w_a_in = din("w_a_in", [1024, 4096])
    a_lb = din("a_lb", [2, 1024])
    a_gnorm = din("a_gnorm", [1, 128])
    w_a_out = din("w_a_out", [1024, 1024])
    g_mix = din("g_mix", [2, 1024])
    g_mlp = din("g_mlp", [2, 1024])
    g_kv = din("g_kv", [1, 1024])
    w_kv = din("w_kv", [1024, 512])
    w_b_q = din("w_b_q", [1024, 1024])
    b_sink = din("b_sink", [1, 16])
    w_b_out = din("w_b_out", [1024, 1024])
    rel_bias = din("rel_bias", [32, 16])
    w_up = din("w_up", [2, 1024, 4096])
    w_down = din("w_down", [2, 4096, 1024])
    g_final = din("g_final", [1, 1024])
    cfd = din("cf_d", [128, NCF])
    t5d = din("t5_d", [32, 1024])

    y_p = dout("y_p", [SEQ, 1024])
    y_s = dout("y_s", [NS, 1024])
    st_p = dout("st_p", [8, 128, 128])
    st_s = dout("st_s", [NS, 8, 128, 128])
    k_p = dout("k_p", [128, 256])
    v_p = dout("v_p", [128, 256])
    k_s = dout("k_s", [NS, 128, 256])
    v_s = dout("v_s", [NS, 128, 256])
    tabD = nc.dram_tensor("tabD", [16, 512], F32, kind="Internal").ap()
    dbg = dout("dbg", [128, 8192]) if fl.get("dbg") else None

    def sb(name, shape, dt):
        return ES.enter_context(nc.sbuf_tensor(name, shape, dt))

    def G():
        return Gran()

    xT = sb("xT", [128, 8, T], F32)
    hT = sb("hT", [128, 8, T], BF16)
    arena = sb("arena", [128, 16 * 1040], BF16)
    TMPA_BYTES = 41 * 1024
    tmpA = sb("tmpA", [128, TMPA_BYTES // 4], F32)
    cf = sb("cf", [128, NCF], F32)
    identb = sb("identb", [128, 128], BF16)
    onesb = sb("onesb", [128, 128], BF16)
    vcol = sb("vcol", [128, 72], F32)
    lbv = sb("lbv", [128, 3, 8], F32)
    esink = sb("esink", [128, 16], F32)
    kvs_full = sb("kvs", [128, 512], F32)
    kvs = kvs_full[0:16, :]
    atok = sb("atok", [128, 1024], BF16)
    NSLOT = 2
    wring = [sb("wr%d" % i, [128, 8 * 512], BF16) for i in range(NSLOT)]
    g_wring = [G() for _ in range(NSLOT)]
    psA = ES.enter_context(nc.psum_tensor("psA", [128, 6, 512], F32))
    psB = ES.enter_context(nc.psum_tensor("psB", [128, 2, 1024], BF16))
    g_psA = [G() for _ in range(6)]
    g_psB = [G() for _ in range(2)]
    g_xT = [[G() for _ in TILES] for _ in range(8)]
    g_hT = [[G() for _ in range(17)] for _ in range(8)]
    g_cf = G()
    g_misc = G()

    ident = cf[:, 0:128]
    hmask = cf[:, 128:256]
    scanmask = cf[:, 256:768]
    mlo = cf[:, 768:769]
    mhi = cf[:, 769:770]

    state = {"bankA": 0, "bankB": 0, "nb": 6, "base": 0}
    phcell = sb("phcell", [128, 2], F32)
    g_phase = G()

    def phase_barrier():
        P.barrier()
        P.add("dve", lambda e: e.memset(phcell[:], 0.0), (), [g_phase])

    def nextA():
        b = state["bankA"] % state["nb"]
        state["bankA"] = (b + 1) % state["nb"]
        return b + state["base"]

    def nextB():
        b = state["bankB"]
        state["bankB"] = (b + 1) % 2
        return b

    def hT_gr(k, t0, n):
        if t0 >= SEQ:
            return [g_hT[k][16]]
        r = [g_hT[k][j] for j in range(t0 // 128, (min(t0 + n, SEQ) + 127) // 128)]
        if t0 + n > SEQ:
            r.append(g_hT[k][16])
        return r

    def hT_all(t0, n):
        r = []
        for k in range(8):
            r += hT_gr(k, t0, n)
        return r

    def mm(out, lhsT, rhs, start, stop, reads, writes):
        P.add("pe", lambda e: e.matmul(out, lhsT, rhs, start=start, stop=stop), reads, writes)

    def tr(out, in_, idn, reads, writes):
        P.add("pe", lambda e: e.transpose(out, in_, idn), reads, writes)

    def act(out, in_, func, reads, writes, scale=None, bias=None):
        kw = {}
        if scale is not None:
            kw["scale"] = scale
        if bias is not None:
            kw["bias"] = bias
        P.add("act", lambda e: e.activation(out, in_, func, **kw), reads, writes)

    def tt(out, in0, in1, op, reads, writes, eng="dve"):
        P.add(eng, lambda e: e.tensor_tensor(out, in0, in1, op), reads, writes)

    def ts(out, in0, s1, s2, op0, op1, reads, writes, eng="dve"):
        if op1 is None:
            P.add(eng, lambda e: e.tensor_scalar(out, in0, s1, None, op0), reads, writes)
        else:
            P.add(eng, lambda e: e.tensor_scalar(out, in0, s1, s2, op0, op1), reads, writes)

    def stt(out, in0, scalar, in1, op0, op1, reads, writes):
        P.add("dve", lambda e: e.scalar_tensor_tensor(out, in0, scalar, in1, op0, op1), reads, writes)

    def cp(out, in_, reads, writes, eng="dve"):
        if eng == "act":
            P.add(eng, lambda e: e.activation(out, in_, AF.Copy), reads, writes)
        else:
            P.add(eng, lambda e: e.tensor_copy(out, in_), reads, writes)

    def mset(ap, val, writes, eng="dve"):
        P.add(eng, lambda e: e.memset(ap, val), (), writes)

    def dma(out, in_, reads, writes, q="sp", is_out=False, **kw):
        return P.add(q, lambda e: e.dma_start(out=out, in_=in_, **kw), reads, writes, dma=True, out=is_out)

    wspecs = []

    def wsrc(kind, a):
        if kind == "ain":
            h = a
            src = w_a_in.rearrange("(k p) (s c) -> p k s c", p=128, s=4)[:, :, :, h * 128:(h + 1) * 128]
            return src, lambda w: w[:, 0:4096].rearrange("p (k s c) -> p k s c", k=8, s=4)
        if kind == "k8":
            mat, c0 = a
            src = mat.rearrange("(k p) c -> p k c", p=128)[:, :, c0:c0 + 512]
            return src, lambda w: w[:, 0:4096].rearrange("p (k c) -> p k c", k=8)
        if kind == "down":
            l, kg, cq = a
            src = w_down[l].rearrange("(k p) c -> p k c", p=128)[:, kg * 16:(kg + 1) * 16, cq * 256:(cq + 1) * 256]
            return src, lambda w: w[:, 0:4096].rearrange("p (k c) -> p k c", k=16)
        raise ValueError(kind)

    wstate = {"issued": 0}

    def w_plan(kind, a):
        wspecs.append((kind, a))
        return len(wspecs) - 1

    def w_use(i, limit=None):
        lim = i + NSLOT - 1 if limit is None else limit
        while wstate["issued"] < len(wspecs) and wstate["issued"] <= lim:
            j = wstate["issued"]
            kind, a = wspecs[j]
            src, view = wsrc(kind, a)
            slot = j % NSLOT
            if kind == "ain":
                for sec in range(4):
                    dma(view(wring[slot])[:, :, sec, :], src[:, :, sec, :], (), [g_wring[slot]], q="pool")
            else:
                dma(view(wring[slot]), src, (), [g_wring[slot]], q="pool")
            wstate["issued"] += 1
        kind, a = wspecs[i]
        _, view = wsrc(kind, a)
        return view(wring[i % NSLOT]), g_wring[i % NSLOT]

    plan = {}
    if fl["hgrn"]:
        plan["ain"] = [w_plan("ain", h) for h in range(8)]
        plan["aout"] = [w_plan("k8", (w_a_out, c)) for c in (0, 512)]

    def plan_mlp(l):
        r = []
        for half in range(2):
            for ffh in range(2):
                ups = [w_plan("k8", (w_up[l], ffh * 2048 + j * 512)) for j in range(4)]
                downs = [w_plan("down", (l, ffh, cq)) for cq in range(4)]
                r.append((ups, downs))
        return r

    if fl["mlp0"]:
        plan["mlp0"] = plan_mlp(0)
    if fl["kv"]:
        plan["kv"] = w_plan("k8", (w_kv, 0))
    if fl["attn"]:
        plan["q"] = [w_plan("k8", (w_b_q, c)) for c in (0, 512)]
        plan["bout"] = [w_plan("k8", (w_b_out, c)) for c in (0, 512)]
    if fl["mlp1"]:
        plan["mlp1"] = plan_mlp(1)

    dma(cf[:], cfd, (), [g_cf])
    cp(identb[:], ident, [g_cf], [g_misc], eng="act")
    mset(onesb[:], 1.0, [g_misc])
    vst = tmpA[0:72, 0:128]
    g_vst = G()
    mset(tmpA[0:72, 0:128], 0.0, [g_vst])
    dma(vst[0:16, :], a_lb.rearrange("r (k p) -> (r k) p", p=128), (), [g_vst])
    dma(vst[16:32, :], g_mix.rearrange("r (k p) -> (r k) p", p=128), (), [g_vst])
    dma(vst[32:48, :], g_mlp.rearrange("r (k p) -> (r k) p", p=128), (), [g_vst])
    dma(vst[48:56, :], g_kv.rearrange("r (k p) -> (r k) p", p=128), (), [g_vst])
    dma(vst[56:64, :], g_final.rearrange("r (k p) -> (r k) p", p=128), (), [g_vst])
    dma(vst[64:65, :], a_gnorm, (), [g_vst])
    b = nextA()
    tr(psA[:, b, 0:72], vst, cf[0:72, 0:72], [g_vst, g_cf], [g_psA[b]])
    g_vcol = G()
    cp(vcol[:], psA[:, b, 0:72], [g_psA[b]], [g_vcol])
    tt(lbv[:, 1, :], vcol[:, 0:8], vcol[:, 8:16], ALU.subtract, [g_vcol], [g_vcol])
    act(lbv[:, 0, :], lbv[:, 1, :], AF.Sigmoid, [g_vcol], [g_vcol])
    ts(lbv[:, 1, :], lbv[:, 0, :], -1.0, 1.0, ALU.mult, ALU.add, [g_vcol], [g_vcol])
    ts(lbv[:, 2, :], lbv[:, 0, :], -1.0, None, ALU.add, None, [g_vcol], [g_vcol])
    GMIX, GMLP, GKV, GFIN, GNORM = 16, 32, 48, 56, 64

    xst = [tmpA[:, 1024 * i + 256:1024 * (i + 1) + 256] for i in range(4)]
    g_xst = [G(), G(), G(), G()]
    for j in range(16):
        s = j % 4
        dma(xst[s], xp[j * 128:(j + 1) * 128, :], (), [g_xst[s]])
        for hk in range(2):
            b = nextA()
            for kk_ in range(4):
                k = hk * 4 + kk_
                tr(psA[:, b, kk_ * 128:(kk_ + 1) * 128], xst[s][:, k * 128:(k + 1) * 128], ident,
                   [g_xst[s], g_cf], [g_psA[b]])
            eng = "act" if hk == 0 else "dve"
            cp(xT[:, hk * 4:hk * 4 + 4, j * 128:(j + 1) * 128],
               psA[:, b, :].rearrange("p (k t) -> p k t", k=4),
               [g_psA[b]], [g_xT[hk * 4 + i][j // 4] for i in range(4)], eng=eng)
    xss = tmpA[0:16, 4352:4352 + 1024]
    g_xss = G()
    dma(xss, xs, (), [g_xss])
    b = nextA()
    for k in range(8):
        tr(psA[:, b, k * 16:(k + 1) * 16], xss[:, k * 128:(k + 1) * 128], cf[0:16, 0:16], [g_xss, g_cf], [g_psA[b]])
    cp(xT[:, :, SEQ:T], psA[:, b, 0:128].rearrange("p (k t) -> p k t", k=8), [g_psA[b]],
       [g_xT[k][4] for k in range(8)])

    nrm_sq = sb("nrm_sq", [128, 2, 512], BF16)
    nrm_r = sb("nrm_r", [128, 2, 512], F32)
    g_nsq = [G(), G()]
    g_nr = [G(), G()]
    nst = {"i": 0}

    def rstd_from_ps(ps_ap, n, inv_d, g_ps):
        i = nst["i"] % 2
        nst["i"] += 1
        r = nrm_r[:, i, 0:n]
        act(r, ps_ap, AF.Ln, [g_ps], [g_nr[i]], scale=inv_d, bias=EPS)
        act(r, r, AF.Exp, [g_nr[i]], [g_nr[i]], scale=-0.5)
        return r, g_nr[i]

    def rmsnorm(gbase, tiles=range(5), out_fn=None):
        for ti in tiles:
            t0, n = TILES[ti]
            b = nextA()
            for k in range(8):
                i = k % 2
                act(nrm_sq[:, i, 0:n], xT[:, k, t0:t0 + n], AF.Square, [g_xT[k][ti]], [g_nsq[i]])
                mm(psA[:, b, 0:n], onesb[:], nrm_sq[:, i, 0:n], k == 0, k == 7, [g_nsq[i], g_misc], [g_psA[b]])
            r, g_r = rstd_from_ps(psA[:, b, 0:n], n, 1.0 / 1024.0, g_psA[b])
            for k in range(8):
                if out_fn is None:
                    stt(hT[:, k, t0:t0 + n], xT[:, k, t0:t0 + n], vcol[:, gbase + k:gbase + k + 1], r,
                        ALU.mult, ALU.mult, [g_xT[k][ti], g_r, g_vcol], hT_gr(k, t0, n))
                else:
                    out_fn(ti, k, r, g_r)

    def proj_fm(wv, g_w, mlist, tiles, consume, src=None, src_gr=None):
        if src is None:
            src, src_gr = hT, hT_gr
        for m in mlist:
            for ti in tiles:
                t0, n = TILES[ti]
                b = nextA()
                for k in range(8):
                    mm(psA[:, b, 0:n], wv[:, k, m * 128:(m + 1) * 128], src[:, k, t0:t0 + n], k == 0, k == 7,
                       [g_w] + src_gr(k, t0, n), [g_psA[b]])
                consume(m, ti, b)

    def add_into_x(mg):
        def f(m, ti, b):
            t0, n = TILES[ti]
            k = mg + m
            tt(xT[:, k, t0:t0 + n], xT[:, k, t0:t0 + n], psA[:, b, 0:n], ALU.add,
               [g_psA[b], g_xT[k][ti]], [g_xT[k][ti]])
        return f


    smp = sb("smp", [128, 5, 8, NS], F32)
    qsb = sb("qsb", [128, 8, NS], BF16)
    decs = sb("decs", [128, 4, 8], F32)
    stcur = sb("stcur", [128, 128], F32)
    cks = sb("cks", [128, 3, 8], F32)
    sgs = sb("sgs", [128, NS], F32)
    vsD = nc.dram_tensor("vsD", [NS, 1024], F32, kind="Internal").ap()

    def hgrn():
        P.barrier()
        rmsnorm(GMIX)
        yT = arena[:, 0:8 * T].rearrange("p (k t) -> p k t", k=8)
        g_y = [[G() for _ in range(5)] for _ in range(8)]
        off = [0]

        def carve(nf32, dt=F32):
            a = tmpA[:, off[0]:off[0] + nf32]
            off[0] += nf32
            return a.bitcast(BF16) if dt == BF16 else a

        names = ["ktil", "kstT", "qtil", "qin", "gate", "vtok"]
        sets = [{nm: carve(256, BF16) for nm in names} for _ in range(2)]
        gs = [{nm: G() for nm in names} for _ in range(2)]
        kvsb = kvs_full[:, :].bitcast(BF16)
        sets.append({"ktil": atok[:, 0:512], "kstT": atok[:, 512:1024], "qtil": kvsb[:, 0:512], "qin": kvsb[:, 512:1024],
                     "gate": nrm_sq[:, 0, :], "vtok": nrm_sq[:, 1, :]})
        gs.append({"ktil": G(), "kstT": G(), "qtil": G(), "qin": G(), "gate": g_nsq[0], "vtok": g_nsq[1]})
        tA, tB, tC, tD, tE, tE2 = [carve(512) for _ in range(6)]
        g_tA, g_tB, g_tC, g_tD, g_tE, g_tE2 = [G() for _ in range(6)]
        kst_lo, kst_hi = carve(256, BF16), carve(256, BF16)
        g_klo, g_khi = G(), G()
        U_sb, Dbc = carve(1024), carve(1024)
        g_U, g_Dbc = G(), G()
        Sbf = [carve(512, BF16), carve(512, BF16)]
        g_Sbf = [G(), G()]
        attT, sqb, grs = carve(256, BF16), carve(256, BF16), carve(256, BF16)
        g_attT, g_sqb, g_grs = G(), G(), G()
        assert off[0] <= TMPA_BYTES // 4
        g_dec = [G(), G()]
        g_dec0 = [G(), G()]
        g_cks = G()
        g_stcur = G()
        g_smp, g_qsb, g_sgs = G(), G(), G()
        mset(decs[:, 2:4, 0:1], 0.0, [g_dec0[0], g_dec0[1]])
        U3 = U_sb.rearrange("p (v c) -> p v c", c=8)
        D3 = Dbc.rearrange("p (v c) -> p v c", c=8)

        def col(i, h):
            return lbv[:, i, h:h + 1]

        def stageA(h, ti, S, D):
            t0 = ti * 512
            wv, g_w = w_use(plan["ain"][h])
            st, g = sets[S], gs[S]
            bf = 0
            for k in range(8):
                mm(psA[:, bf, :], wv[:, k, 1, :], hT[:, k, t0:t0 + 512], k == 0, k == 7, [g_w] + hT_gr(k, t0, 512), [g_psA[bf]])
            act(tA, psA[:, bf, :], AF.Sigmoid, [g_psA[bf]], [g_tA])
            bg = 1
            for k in range(8):
                mm(psA[:, bg, :], wv[:, k, 3, :], hT[:, k, t0:t0 + 512], k == 0, k == 7, [g_w] + hT_gr(k, t0, 512), [g_psA[bg]])
            act(tE2, psA[:, bg, :], AF.Sigmoid, [g_psA[bg]], [g_tE2])
            tt(st["gate"], psA[:, bg, :], tE2, ALU.mult, [g_psA[bg], g_tE2], [g["gate"]])
            act(tB, tA, AF.Ln, [g_tA, g_vcol], [g_tB], scale=col(1, h), bias=col(0, h))
            act(tD, tA, AF.Identity, [g_tA, g_vcol], [g_tD], scale=col(2, h), bias=col(1, h))
            P.add("dve", lambda e: e.tensor_tensor_scan(tC, scanmask, tB, 0.0, ALU.mult, ALU.add), [g_tB, g_cf], [g_tC])
            tC3 = tC.rearrange("p (c s) -> p c s", s=64)
            tt(tA.rearrange("p (c s) -> p c s", s=64), tC3, tC3[:, :, 31:32].broadcast_to([128, 8, 64]), ALU.subtract, [g_tC], [g_tA])
            tt(cks[:, 2, :], tC[:, 63:512:64], tC[:, 31:512:64], ALU.subtract, [g_tC], [g_cks])
            act(cks[:, 0, :], cks[:, 2, :], AF.Exp, [g_cks], [g_cks])
            act(cks[:, 1, :], tC[:, 31:512:64], AF.Exp, [g_tC], [g_cks])
            act(decs[:, D, :], tC[:, 63:512:64], AF.Exp, [g_tC], [g_dec[D]])
            act(tE, tA, AF.Exp, [g_tA], [g_tE], scale=-1.0)
            tt(st["ktil"], tD, tE, ALU.mult, [g_tD, g_tE], [g["ktil"]])
            tt(st["kstT"].rearrange("p (c s) -> p c s", s=64), st["ktil"].rearrange("p (c s) -> p c s", s=64),
               cks[:, 0, :].unsqueeze(2).broadcast_to([128, 8, 64]), ALU.mult, [g["ktil"], g_cks], [g["kstT"]])
            act(tA, tA, AF.Exp, [g_tA], [g_tA])
            bq = 0
            for k in range(8):
                mm(psA[:, bq, :], wv[:, k, 0, :], hT[:, k, t0:t0 + 512], k == 0, k == 7, [g_w] + hT_gr(k, t0, 512), [g_psA[bq]])
            tt(st["qtil"], psA[:, bq, :], tA, ALU.mult, [g_psA[bq], g_tA], [g["qtil"]])
            tt(st["qin"].rearrange("p (c s) -> p c s", s=64), st["qtil"].rearrange("p (c s) -> p c s", s=64),
               cks[:, 1, :].unsqueeze(2).broadcast_to([128, 8, 64]), ALU.mult, [g["qtil"], g_cks], [g["qin"]])
            bv = 1
            for sub in range(4):
                for k in range(8):
                    mm(psA[:, bv, sub * 128:(sub + 1) * 128], hT[:, k, t0 + sub * 128:t0 + (sub + 1) * 128], wv[:, k, 2, :],
                       k == 0, k == 7, [g_w, g_hT[k][ti * 4 + sub]], [g_psA[bv]])
            cp(st["vtok"], psA[:, bv, :], [g_psA[bv]], [g["vtok"]], eng="act")

        def stageS(h):
            wv, g_w = w_use(plan["ain"][h])
            for sec in (1, 0, 3, 2):
                b = {1: 0, 0: 1, 3: 0, 2: 1}[sec]
                for k in range(8):
                    mm(psA[:, b, 0:NS], wv[:, k, sec, :], hT[:, k, SEQ:T], k == 0, k == 7, [g_w, g_hT[k][16]], [g_psA[b]])
                pv = psA[:, b, 0:NS]
                if sec == 1:
                    act(sgs[:], pv, AF.Sigmoid, [g_psA[b]], [g_sgs])
                    act(smp[:, 0, h, :], sgs[:], AF.Identity, [g_sgs, g_vcol], [g_smp], scale=col(1, h), bias=col(0, h))
                    act(smp[:, 1, h, :], sgs[:], AF.Identity, [g_sgs, g_vcol], [g_smp], scale=col(2, h), bias=col(1, h))
                elif sec == 0:
                    cp(qsb[:, h, :], pv, [g_psA[b]], [g_qsb], eng="dve")
                elif sec == 3:
                    act(sgs[:], pv, AF.Sigmoid, [g_psA[b]], [g_sgs])
                    tt(smp[:, 2, h, :], pv, sgs[:], ALU.mult, [g_psA[b], g_sgs], [g_smp])
                else:
                    cp(smp[:, 3, h, :], pv, [g_psA[b]], [g_smp], eng="dve")

        def stageB(h, ti, S, D, SB):
            st, g = sets[S], gs[S]
            bB = nextB()
            for sub in range(4):
                tr(psB[:, bB, sub * 128:(sub + 1) * 128], st["kstT"][:, sub * 128:(sub + 1) * 128], identb[:],
                   [g["kstT"], g_misc], [g_psB[bB]])
            act(kst_lo, psB[:, bB, 0:512], AF.Copy, [g_psB[bB], g_cf], [g_klo], scale=mlo)
            act(kst_hi, psB[:, bB, 0:512], AF.Copy, [g_psB[bB], g_cf], [g_khi], scale=mhi)
            for hb in range(2):
                bu = 2 + hb
                for cc in range(4):
                    c = hb * 4 + cc
                    sub, half = c // 2, c % 2
                    ks, gk = (kst_lo, g_klo) if half == 0 else (kst_hi, g_khi)
                    mm(psA[:, bu, cc * 128:(cc + 1) * 128], ks[:, sub * 128:(sub + 1) * 128],
                       st["vtok"][:, sub * 128:(sub + 1) * 128], True, True, [gk, g["vtok"]], [g_psA[bu]])
                cp(U3[:, :, hb * 4:(hb + 1) * 4].transpose([0, 2, 1]), psA[:, bu, :].rearrange("p (c v) -> p c v", c=4),
                   [g_psA[bu]], [g_U], eng=("dve" if hb == 0 else "act"))
            if ti > 0:
                stt(U3[:, :, 0], stcur[:], decs[:, D, 0:1], U3[:, :, 0], ALU.mult, ALU.add, [g_stcur, g_dec[D], g_U], [g_U])
            cp(decs[:, 2 + D, 1:8], decs[:, D, 1:8], [g_dec[D]], [g_dec0[D]], eng="dve")
            act(D3, decs[:, 2 + D, :].unsqueeze(1).broadcast_to([128, 128, 8]), AF.Copy, [g_dec0[D]], [g_Dbc])
            P.add("dve", lambda e: e.tensor_tensor_scan(U_sb, Dbc, U_sb, 0.0, ALU.mult, ALU.add), [g_Dbc, g_U], [g_U])
            cp(stcur[:], U3[:, :, 7], [g_U], [g_stcur], eng="dve")
            sb3 = Sbf[SB].rearrange("p (c v) -> p c v", c=8)
            u3t = U3.transpose([0, 2, 1])
            cp(sb3[:, 0:4, :], u3t[:, 0:4, :], [g_U], [g_Sbf[SB]], eng="dve")
            cp(sb3[:, 4:8, :], u3t[:, 4:8, :], [g_U], [g_Sbf[SB]], eng="act")
            if ti == 3:
                dma(st_p[h], stcur[:], [g_stcur], [g_stcur], is_out=True)

        def stageC(h, ti, S, SB):
            st, g = sets[S], gs[S]
            t0 = ti * 512
            ba = 4
            for sub in range(4):
                mm(psA[:, ba, sub * 128:(sub + 1) * 128], st["ktil"][:, sub * 128:(sub + 1) * 128],
                   st["qtil"][:, sub * 128:(sub + 1) * 128], True, True, [g["ktil"], g["qtil"]], [g_psA[ba]])
            tt(attT.rearrange("p (a t) -> p a t", a=4), psA[:, ba, :].rearrange("p (a t) -> p a t", a=4),
               hmask.unsqueeze(1).broadcast_to([128, 4, 128]), ALU.mult, [g_psA[ba], g_cf], [g_attT])
            bo = 5
            for sub in range(4):
                mm(psA[:, bo, sub * 128:(sub + 1) * 128], st["vtok"][:, sub * 128:(sub + 1) * 128],
                   attT[:, sub * 128:(sub + 1) * 128], True, False, [g["vtok"], g_attT], [g_psA[bo]])
                for half in range(2):
                    c = 2 * sub + half
                    if ti == 0 and c == 0:
                        continue
                    if c == 0:
                        lhs, gl = Sbf[1 - SB][:, 7 * 128:8 * 128], g_Sbf[1 - SB]
                    else:
                        lhs, gl = Sbf[SB][:, (c - 1) * 128:c * 128], g_Sbf[SB]
                    mm(psA[:, bo, c * 64:(c + 1) * 64], lhs, st["qin"][:, c * 64:(c + 1) * 64], False, half == 1,
                       [gl, g["qin"]], [g_psA[bo]])
            act(sqb, psA[:, bo, :], AF.Square, [g_psA[bo]], [g_sqb])
            bs = 4
            mm(psA[:, bs, :], onesb[:], sqb, True, True, [g_sqb, g_misc], [g_psA[bs]])
            r, g_r = rstd_from_ps(psA[:, bs, :], 512, 1.0 / 128.0, g_psA[bs])
            tt(grs, st["gate"], r, ALU.mult, [g["gate"], g_r], [g_grs])
            stt(yT[:, h, t0:t0 + 512], psA[:, bo, :], vcol[:, GNORM:GNORM + 1], grs, ALU.mult, ALU.mult,
                [g_psA[bo], g_grs, g_vcol], [g_y[h][ti]])

        its = [(h, ti) for h in range(8) for ti in range(4)]
        N_IT = len(its)
        for step in range(-2, N_IT):
            lists = []
            ia = step + 2
            P.record()
            if ia <= N_IT and fl["sample"]:
                if ia == N_IT:
                    stageS(7)
                elif its[ia][1] == 0 and its[ia][0] > 0:
                    stageS(its[ia][0] - 1)
            if ia < N_IT:
                stageA(its[ia][0], its[ia][1], ia % 3, ia % 2)
            la = P.stop()
            if la:
                lists.append(la)
            ib = step + 1
            if 0 <= ib < N_IT:
                P.record()
                stageB(its[ib][0], its[ib][1], ib % 3, ib % 2, ib % 2)
                lists.append(P.stop())
            if 0 <= step < N_IT:
                P.record()
                stageC(its[step][0], its[step][1], step % 3, step % 2)
                lists.append(P.stop())
            P.replay(*lists)

        phase_barrier()

        def y_gr(k, t0, n):
            return [g_y[k][t0 // 512]]

        i0_, i1_ = plan["aout"]
        aw = [w_use(i0_, limit=i1_), w_use(i1_, limit=i1_)]
        state["nb"] = 4
        state["bankA"] = 0
        P.record()
        for si in range(2):
            proj_fm(aw[si][0], aw[si][1], range(4), range(4), add_into_x(si * 4), src=yT, src_gr=y_gr)
        l_proj = P.stop()
        P.record()
        if fl["sample"]:
            stS = [tmpA[:, 2048 * i:2048 * (i + 1)].rearrange("p (s h v) -> p s h v", s=2, h=8) for i in range(3)]
            g_stS = [G(), G(), G()]
            vbc = [tmpA[:, 6144 + 1024 * i:6144 + 1024 * (i + 1)] for i in range(4)]
            g_vbc = [G(), G(), G(), G()]
            vtk = nrm_r[0:16, :, :].rearrange("p a b -> p (a b)")
            b0, b1 = 4, 5
            for hh_ in range(8):
                bb = b0 if hh_ < 4 else b1
                tr(psA[0:16, bb, (hh_ % 4) * 128:(hh_ % 4 + 1) * 128], smp[:, 3, hh_, :], ident, [g_smp, g_cf], [g_psA[bb]])
            cp(vtk[:, 0:512], psA[0:16, b0, :], [g_psA[b0]], [g_nr[0]], eng="dve")
            cp(vtk[:, 512:1024], psA[0:16, b1, :], [g_psA[b1]], [g_nr[1]], eng="dve")
            g_vsD = G()
            dma(vsD, vtk, [g_nr[0], g_nr[1]], [g_vsD])
            bo = 5
            def ld_state(sg):
                dma(stS[sg % 3], st_in[sg * 2:(sg + 1) * 2].rearrange("s h k v -> k s h v"), [g_phase], [g_stS[sg % 3]])

            def ld_vbc(s_):
                dma(vbc[s_ % 4], vsD[s_:s_ + 1, :].broadcast_to([128, 1024]), [g_vsD, g_phase], [g_vbc[s_ % 4]], q="act")
            ld_state(0)
            ld_state(1)
            for s_ in range(3):
                ld_vbc(s_)
            stbf3 = nrm_sq[:, :, :].rearrange("p a (h v) -> p (a h) v", v=128)
            def smm(s_):
                for h in range(8):
                    mm(psA[:, bo, h * 16 + s_:h * 16 + s_ + 1], stbf3[:, h, :], qsb[:, h, s_:s_ + 1], True, True,
                       [g_nsq[0], g_nsq[1], g_qsb], [g_psA[bo]])

            for sg in range(8):
                bi = sg % 3
                if sg + 2 < 8:
                    ld_state(sg + 2)
                for si_ in range(2):
                    s_ = sg * 2 + si_
                    vi = s_ % 4
                    if s_ + 3 < NS:
                        ld_vbc(s_ + 3)
                    vb3 = vbc[vi].rearrange("p (h v) -> p h v", h=8)
                    st3 = stS[bi][:, si_, :, :]
                    tt(vb3, vb3, smp[:, 1, :, s_].unsqueeze(2).broadcast_to([128, 8, 128]), ALU.mult,
                       [g_vbc[vi], g_smp], [g_vbc[vi]])
                    tt(st3, st3, smp[:, 0, :, s_].unsqueeze(2).broadcast_to([128, 8, 128]), ALU.mult,
                       [g_stS[bi], g_smp], [g_stS[bi]])
                    tt(st3, st3, vb3, ALU.add, [g_stS[bi], g_vbc[vi]], [g_stS[bi]])
                    if s_ > 0:
                        smm(s_ - 1)
                    act(stbf3, st3, AF.Copy, [g_stS[bi]], [g_nsq[0], g_nsq[1]])
                    if s_ == NS - 1:
                        smm(s_)
                dma(st_s[sg * 2:(sg + 1) * 2].rearrange("s h k v -> k s h v"), stS[bi], [g_stS[bi]], [g_stS[bi]], is_out=True)
            sq2 = sqb[:, 0:128]
            act(sq2, psA[:, bo, 0:128], AF.Square, [g_psA[bo]], [g_sqb])
            bs = 4
            mm(psA[:, bs, 0:128], onesb[:], sq2, True, True, [g_sqb, g_misc], [g_psA[bs]])
            r, g_r = rstd_from_ps(psA[:, bs, 0:128], 128, 1.0 / 128.0, g_psA[bs])
            g_gr2 = G()
            gr2 = tmpA[:, 10240:10240 + 128]
            tt(gr2, smp[:, 2, :, :].rearrange("p h s -> p (h s)"), r, ALU.mult, [g_smp, g_r], [g_gr2])
            stt(yT[:, :, SEQ:T], psA[:, bo, 0:128].rearrange("p (h s) -> p h s", h=8), vcol[:, GNORM:GNORM + 1],
                gr2.rearrange("p (h s) -> p h s", h=8), ALU.mult, ALU.mult, [g_psA[bo], g_gr2, g_vcol],
                [g_y[k][4] for k in range(8)])
        else:
            mset(yT[:, :, SEQ:T], 0.0, [g_y[k][4] for k in range(8)])
            mset(stcur[:], 0.0, [g_stcur])
            for sg in range(4):
                dma(st_s[sg * 4:(sg + 1) * 4].rearrange("s h k v -> k (s h) v"),
                    stcur[:].unsqueeze(1).broadcast_to([128, 32, 128]), [g_stcur], [], is_out=True)
        l_smp = P.stop()
        P.replay(l_proj, l_smp)
        state["nb"] = 6
        for si in range(2):
            proj_fm(aw[si][0], aw[si][1], range(4), [4], add_into_x(si * 4), src=yT, src_gr=y_gr)

    if fl["hgrn"]:
        hgrn()

    MT = [(0, 512), (512, 512), (1024, 512), (1536, 264), (1800, 264)]

    def xg(k, t0, n):
        r = []
        for ti, (a, w) in enumerate(TILES):
            if a < t0 + n and t0 < a + w:
                r.append(g_xT[k][ti])
        return r

    def mlp(l, wplan, tail=None):
        P.barrier()
        rmsnorm(GMLP + 8 * l)
        uT = arena[:, :].rearrange("p (c t) -> p c t", c=16)
        g_u = [[G() for _ in range(3)] for _ in range(16)]
        rl = [tmpA[:, 512 * i:512 * (i + 1)] for i in range(2)]
        g_rl = [G(), G()]
        rst = {"i": 0}
        idx = 0
        for half in range(2):
            tl = [0, 1] if half == 0 else [2, 3, 4]
            base = MT[tl[0]][0]
            if half == 1 and tail is not None:
                state["nb"], state["bankA"] = 4, 0
                P.record()
            for ffh in range(2):
                ups, downs = wplan[idx]
                idx += 1
                for j, wi in enumerate(ups):
                    wv, g_w = w_use(wi)
                    for m in range(4):
                        for ti in tl:
                            t0, n = MT[ti]
                            b = nextA()
                            for k in range(8):
                                mm(psA[:, b, 0:n], wv[:, k, m * 128:(m + 1) * 128], hT[:, k, t0:t0 + n], k == 0, k == 7,
                                   [g_w] + hT_gr(k, t0, n), [g_psA[b]])
                            i = rst["i"] % 2
                            rst["i"] += 1
                            c = j * 4 + m
                            act(rl[i][:, 0:n], psA[:, b, 0:n], AF.Relu, [g_psA[b]], [g_rl[i]])
                            tt(uT[:, c, t0 - base:t0 - base + n], rl[i][:, 0:n], rl[i][:, 0:n], ALU.mult,
                               [g_rl[i]], [g_u[c][tl.index(ti)]])
                for cq, wi in enumerate(downs):
                    wv, g_w = w_use(wi)
                    for ti in tl:
                        t0, n = MT[ti]
                        for m in range(2):
                            b = nextA()
                            for k in range(16):
                                mm(psA[:, b, 0:n], wv[:, k, m * 128:(m + 1) * 128],
                                   uT[:, k, t0 - base:t0 - base + n], k == 0, k == 15,
                                   [g_w, g_u[k][tl.index(ti)]], [g_psA[b]])
                            kx = cq * 2 + m
                            tt(xT[:, kx, t0:t0 + n], xT[:, kx, t0:t0 + n], psA[:, b, 0:n], ALU.add,
                               [g_psA[b]] + xg(kx, t0, n), xg(kx, t0, n))
            if half == 1 and tail is not None:
                l_b = P.stop()
                state["nb"], state["bankA"], state["base"] = 2, 0, 4
                P.record()
                tail()
                l_t = P.stop()
                state["nb"], state["bankA"], state["base"] = 6, 0, 0
                P.replay(l_b, l_t)

    if fl["mlp0"]:
        mlp(0, plan["mlp0"])

    g_kvs = G()
    relsb = sb("relsb", [32, 16], F32)
    rden = sb("rden", [128, 2, 4], F32)
    sbias = sb("sbias", [128, 16], F32)
    selb = sb("selb", [16, 2, 128], BF16)

    def kv_attn():
        P.barrier()
        kT2 = tmpA[:, 4096:8192].bitcast(BF16).rearrange("p (g t) -> p g t", g=4)
        vatt = tmpA[:, 8192:10304].bitcast(BF16).rearrange("p (j g d) -> p j g d", j=16, g=4)
        g_bias, g_rel, g_tab, g_vones, g_es = G(), G(), G(), G(), G()
        g_kT2 = [[G() for _ in range(4)] for _ in range(4)]
        g_kT2d = [G() for _ in range(4)]
        g_vatt = [G() for _ in range(16)]
        stage = nrm_r[0:32, :, :].rearrange("p a b -> p (a b)")
        dma(stage, t5d, (), [g_nr[0], g_nr[1]])
        dma(relsb[:], rel_bias, (), [g_rel])
        relh = arena[0:32, 0:16]
        rell = arena[0:32, 16:32]
        relt = arena[0:32, 32:64].bitcast(F32)
        ohb = arena[0:32, 64:576]
        g_rl2 = G()
        cp(relh, relsb[:], [g_rel], [g_rl2], eng="dve")
        tt(relt, relsb[:], relh, ALU.subtract, [g_rel, g_rl2], [g_rl2])
        cp(rell, relt, [g_rl2], [g_rl2], eng="dve")
        cp(ohb, stage[:, 0:512], [g_nr[0], g_nr[1]], [g_rl2], eng="dve")
        b = nextA()
        mm(psA[0:16, b, :], relh, ohb, True, False, [g_rl2], [g_psA[b]])
        mm(psA[0:16, b, :], rell, ohb, False, True, [g_rl2], [g_psA[b]])
        tt(kvs[0:16, :], psA[0:16, b, :], stage[0:16, 512:1024], ALU.add, [g_psA[b], g_nr[0], g_nr[1]], [g_kvs])
        if "tab" not in SKIP:
            dma(tabD, kvs[0:16, :], [g_kvs], [g_tab])
        biasR = tmpA[:, 4096:8192].rearrange("p (h b q) -> p h b q", h=16, b=2)
        g_biasR = G()
        for kb in range(2):
            src = bass.AP(tabD.tensor, kb * 256, [[1, 128], [512, 16], [1, 128]])
            if "toep" not in SKIP:
                dma(biasR[:, :, kb, :], src, [g_tab], [g_biasR])
        bRh = arena[:, 1024:1024 + 4096]
        bRl = arena[:, 5120:5120 + 4096]
        bRt = tmpA[:, 0:4096]
        Jb = arena[:, 9216:9216 + 128]
        g_bRh, g_bRl, g_bRt, g_Jb = G(), G(), G(), G()
        cp(Jb, cf[:, 770:898], [g_cf], [g_Jb], eng="act")
        cp(bRh, tmpA[:, 4096:8192], [g_biasR], [g_bRh], eng="act")
        tt(bRt, tmpA[:, 4096:8192], bRh, ALU.subtract, [g_biasR, g_bRh], [g_bRt])
        cp(bRl, bRt, [g_bRt], [g_bRl], eng="act")
        biasT = tmpA[:, 0:4096].rearrange("p (h b q) -> p h b q", h=16, b=2)
        for c in range(8):
            b = nextA()
            mm(psA[:, b, :], Jb, bRh[:, c * 512:(c + 1) * 512], True, False, [g_Jb, g_bRh], [g_psA[b]])
            mm(psA[:, b, :], Jb, bRl[:, c * 512:(c + 1) * 512], False, True, [g_Jb, g_bRl], [g_psA[b]])
            cp(tmpA[:, c * 512:(c + 1) * 512], psA[:, b, :], [g_psA[b]], [g_bias, g_bRt], eng=("act" if c % 2 else "dve"))
        cp(sbias[:], biasT[:, :, 1, 127], [g_bias], [g_bias], eng="dve")
        if "esink" not in SKIP:
            dma(esink[:], b_sink.broadcast_to([128, 16]), (), [g_es])
        act(esink[:], esink[:], AF.Exp, [g_es], [g_es])
        P.barrier()

        if "afterT5" in SKIP:
            return
        rmsnorm(GKV)
        wv, g_w = w_use(plan["kv"])
        for m in range(2):
            for ti in range(4):
                t0, n = TILES[ti]
                b = nextA()
                for k in range(8):
                    mm(psA[:, b, :], wv[:, k, m * 128:(m + 1) * 128], hT[:, k, t0:t0 + 512], k == 0, k == 7,
                       [g_w] + hT_gr(k, t0, n), [g_psA[b]])
                cp(kT2[0:64, 2 * m, t0:t0 + 512], psA[0:64, b, :], [g_psA[b]], [g_kT2[2 * m][ti]], eng="act")
                cp(kT2[64:128, 2 * m + 1, t0:t0 + 512], psA[64:128, b, :], [g_psA[b]], [g_kT2[2 * m + 1][ti]], eng="act")
        for g in range(4):
            if "dup" in SKIP:
                break
            if g % 2 == 0:
                dma(kT2[64:128, g, :], kT2[0:64, g, :], g_kT2[g], [g_kT2d[g]])
            else:
                dma(kT2[0:64, g, :], kT2[64:128, g, :], g_kT2[g], [g_kT2d[g]])
        if "afterK" in SKIP:
            return
        if "vones" not in SKIP:
            act(vatt[:, :, :, 64], onesb[:, 0:64].rearrange("p (j g) -> p j g", j=16), AF.Copy, [g_misc], [g_vones])
        for j in range(16):
            if "vmm" in SKIP:
                break
            if j == 15 and "vj15" in SKIP:
                break
            b = nextA()
            if j < 15:
                for k in range(8):
                    mm(psA[:, b, 0:256], hT[:, k, j * 128:(j + 1) * 128], wv[:, k, 256:512], k == 0, k == 7,
                       [g_w, g_hT[k][j]], [g_psA[b]])
                cp(vatt[:, j, :, 0:64], psA[:, b, 0:256].rearrange("p (g d) -> p g d", g=4), [g_psA[b]], [g_vatt[j]], eng="act")
            else:
                for k in range(8):
                    mm(psA[:, b, :], hT[:, k, j * 128:(j + 1) * 128], wv[:, k, :], k == 0, k == 7,
                       [g_w, g_hT[k][j]], [g_psA[b]])
                cp(nrm_r[:, 0, :], psA[:, b, :], [g_psA[b]], [g_nr[0]], eng="dve")
                cp(vatt[:, j, :, 0:64], nrm_r[:, 0, 256:512].rearrange("p (g d) -> p g d", g=4), [g_nr[0]], [g_vatt[j]], eng="act")
                dma(k_p, nrm_r[:, 0, 0:256], [g_nr[0]], [g_nr[0]], is_out=True)
                dma(v_p, nrm_r[:, 0, 256:512], [g_nr[0]], [g_nr[0]], is_out=True)
        if "afterV" in SKIP:
            return
        b = nextA()
        for k in range(8):
            mm(psA[0:16, b, :], hT[:, k, SEQ:T], wv[:, k, :], k == 0, k == 7, [g_w, g_hT[k][16]], [g_psA[b]])
        cp(kvs[0:16, :], psA[0:16, b, :], [g_psA[b]], [g_kvs], eng="dve")
        dma(k_s[:, 127, :], kvs[0:16, 0:256], [g_kvs], [], is_out=True)
        dma(v_s[:, 127, :], kvs[0:16, 256:512], [g_kvs], [], is_out=True)
        if "d2d" not in SKIP:
            dma(k_s[:, 0:127, :], ck[:, 1:128, :], (), [], is_out=True)
            dma(v_s[:, 0:127, :], cv[:, 1:128, :], (), [], is_out=True)
        if not fl["attn"]:
            return

        rmsnorm(GMIX + 8)
        qT = arena[:, 0:8 * T].rearrange("p (k t) -> p k t", k=8)
        g_q = [[G() for _ in range(5)] for _ in range(8)]
        for si, wi in enumerate(plan["q"]):
            wv, g_w = w_use(wi)

            def cons(m, ti, b, si=si):
                t0, n = TILES[ti]
                j = si * 4 + m
                act(qT[:, j, t0:t0 + n], psA[:, b, 0:n], AF.Copy, [g_psA[b]], [g_q[j][ti]], scale=ATTN_SCALE)
            proj_fm(wv, g_w, range(4), range(5), cons)

        pTb = [nrm_sq[:, 0, :], nrm_sq[:, 1, :]] + [nrm_r[:, i, :].bitcast(BF16)[:, j * 512:(j + 1) * 512]
                                                      for i in range(2) for j in range(2)]
        g_pT = [G() for _ in range(6)]
        pT_first = {2: g_nr[0], 3: g_nr[0], 4: g_nr[1], 5: g_nr[1], 0: g_nsq[0], 1: g_nsq[1]}
        g_atok = [G() for _ in range(4)]
        g_rden = [G(), G()]
        pT_of = {}

        def att1(n, g):
            ti = n // 4
            q0 = n * 128
            kbs = [1] if n == 0 else [0, 1]
            c0 = 256 if n == 0 else 0
            banks = [nextA(), nextA()]
            pTs = []
            nk = len(kbs)
            for hh in range(2):
                b = banks[hh]
                for jj in range(2):
                    j = 2 * g + jj
                    for kb in kbs:
                        kblk = n - 1 + kb
                        kg = g_kT2[g][kblk // 4] if hh == g % 2 else g_kT2d[g]
                        col = (kb * 2 + jj) * 128
                        mm(psA[:, b, col:col + 128], kT2[hh * 64:(hh + 1) * 64, g, kblk * 128:(kblk + 1) * 128],
                           qT[hh * 64:(hh + 1) * 64, j, q0:q0 + 128], True, True, [kg, g_q[j][ti]], [g_psA[b]])
            for hh in range(2):
                b = banks[hh]
                bv = biasT[:, 4 * g + hh:4 * g + hh + 3:2, :, :].transpose([0, 2, 1, 3])
                if n == 0:
                    bv = bv[:, 1:2]
                pv4 = psA[:, b, c0:512].rearrange("p (a j q) -> p a j q", a=nk, j=2)
                tt(pv4, pv4, bv, ALU.add, [g_psA[b], g_bias], [g_psA[b]])
                pi = (2 * (n * 4 + g) + hh) % 6
                wl = [g_pT[pi]]
                if pi in pT_first:
                    wl.append(pT_first.pop(pi))
                act(pTb[pi][:, c0:512], psA[:, b, c0:512], AF.Exp, [g_psA[b]], wl)
                pTs.append((pTb[pi], g_pT[pi]))
            pT_of[(n, g)] = pTs

        def att2(n, g):
            q0 = n * 128
            kbs = [1] if n == 0 else [0, 1]
            pTs = pT_of.pop((n, g))
            bo = nextA()
            for hi in range(4):
                hh, jj = hi % 2, hi // 2
                for kb in kbs:
                    kblk = n - 1 + kb
                    col = (kb * 2 + jj) * 128
                    mm(psA[:, bo, hi * 65:(hi + 1) * 65], pTs[hh][0][:, col:col + 128], vatt[:, kblk, g, 0:65],
                       kb == kbs[0], kb == 1, [pTs[hh][1], g_vatt[kblk], g_vones], [g_psA[bo]])
            ov = psA[:, bo, 0:260].rearrange("p (h d) -> p h d", h=4)
            ri = (n * 4 + g) % 2
            tt(rden[:, ri, :], ov[:, :, 64], esink[:, 4 * g:4 * g + 4], ALU.add, [g_psA[bo], g_es], [g_rden[ri]])
            P.add("dve", lambda e, ri=ri: e.reciprocal(rden[:, ri, :], rden[:, ri, :]), [g_rden[ri]], [g_rden[ri]])
            tt(atok[:, g * 256:(g + 1) * 256].rearrange("p (h d) -> p h d", h=4), ov[:, :, 0:64],
               rden[:, ri, :].unsqueeze(2).broadcast_to([128, 4, 64]), ALU.mult,
               [g_psA[bo], g_rden[ri]], [g_atok[g]])
            if g == 3:
                bB = nextB()
                for j in range(8):
                    tr(psB[:, bB, j * 128:(j + 1) * 128], atok[:, j * 128:(j + 1) * 128], identb[:],
                       [g_atok[j // 2], g_misc], [g_psB[bB]])
                cp(hT[:, :, q0:q0 + 128], psB[:, bB, :].rearrange("p (k t) -> p k t", k=8), [g_psB[bB]],
                   [g_hT[k][n] for k in range(8)], eng="act")

        ngs = [(n, g) for n in range(16) for g in range(4)]
        att1(*ngs[0])
        att1(*ngs[1])
        for i, ng in enumerate(ngs):
            if i + 2 < len(ngs):
                att1(*ngs[i + 2])
            att2(*ng)

        phase_barrier()
        j0_, j1_ = plan["bout"]
        bw = [w_use(j0_, limit=j1_), w_use(j1_, limit=j1_)]
        state["nb"] = 3
        state["bankA"] = 0
        P.record()
        for si in range(2):
            proj_fm(bw[si][0], bw[si][1], range(4), range(4), add_into_x(si * 4))
        l_proj = P.stop()
        P.record()
        Knew = tmpA[:, 0:4096].rearrange("p (s c) -> p s c", s=16)
        Vnew = tmpA[:, 4096:8192].rearrange("p (s c) -> p s c", s=16)
        Vnb = tmpA[:, 8192:10240].bitcast(BF16).rearrange("p (s c) -> p s c", s=16)
        g_Kn, g_Vn, g_Vnb = G(), G(), G()
        dma(Knew[0:127, :, :], ck[:, 1:128, :].rearrange("s r c -> r s c"), [g_phase], [g_Kn])
        dma(Knew[127:128, :, :], kvs[0:16, 0:256], [g_kvs, g_phase], [g_Kn])
        dma(Vnew[0:127, :, :], cv[:, 1:128, :].rearrange("s r c -> r s c"), [g_phase], [g_Vn])
        dma(Vnew[127:128, :, :], kvs[0:16, 256:512], [g_kvs, g_phase], [g_Vn])
        cp(Vnb, Vnew, [g_Vn], [g_Vnb], eng="act")
        qtok = arena[0:16, T:T + 1024]
        g_qtok = G()
        bB = nextB()
        for j in range(8):
            tr(psB[0:16, bB, j * 128:(j + 1) * 128], qT[:, j, SEQ:T], identb[:], [g_q[j][4], g_misc], [g_psB[bB]])
        cp(qtok, psB[0:16, bB, :], [g_psB[bB]], [g_qtok], eng="dve")
        prod = arena[:, 0:2048].bitcast(F32)
        g_prod = G()
        sS = arena[:, 2 * T:2 * T + 512].bitcast(F32)
        pS = arena[:, 3 * T:3 * T + 512].bitcast(F32)
        pSb = arena[:, 4 * T:4 * T + 256]
        g_sS, g_pS, g_pSb, g_sel = G(), G(), G(), [G(), G()]
        for s_ in range(16):
            si = s_ % 2
            cp(selb[:, si, :], identb[0:16, s_:s_ + 1].broadcast_to([16, 128]), [g_misc], [g_sel[si]], eng="dve")
            b0 = 3
            for hf in range(2):
                mm(psA[:, b0 + hf, :], selb[:, si, :], qtok[:, hf * 512:(hf + 1) * 512], True, True,
                   [g_sel[si], g_qtok], [g_psA[b0 + hf]])
            tt(prod.rearrange("p (g j d) -> p g j d", g=4, j=4),
               Knew[:, s_, :].rearrange("p (g d) -> p g d", g=4).unsqueeze(2).broadcast_to([128, 4, 4, 64]),
               psA[:, b0:b0 + 2, :].rearrange("p b (j d) -> p (b j) d", d=64).rearrange("p (g j) d -> p g j d", g=4),
               ALU.mult, [g_Kn, g_psA[b0], g_psA[b0 + 1]], [g_prod])
            P.add("dve", lambda e, s_=s_: e.tensor_reduce(sS[:, s_ * 16:(s_ + 1) * 16],
                                                           prod.rearrange("p (h d) -> p h d", d=64),
                                                           mybir.AxisListType.X, ALU.add), [g_prod], [g_sS])
        tt(sS.rearrange("p (s h) -> p s h", s=16), sS.rearrange("p (s h) -> p s h", s=16),
           sbias[:].unsqueeze(1).broadcast_to([128, 16, 16]), ALU.add, [g_sS, g_bias], [g_sS])
        act(pS, sS, AF.Exp, [g_sS], [g_pS])
        cp(pSb, pS, [g_pS], [g_pSb], eng="dve")
        b = 5
        mm(psA[:, b, 0:256], onesb[:], pSb, True, True, [g_pSb, g_misc], [g_psA[b]])
        tt(sS.rearrange("p (s h) -> p s h", s=16), psA[:, b, 0:256].rearrange("p (s h) -> p s h", s=16),
           esink[:].unsqueeze(1).broadcast_to([128, 16, 16]), ALU.add, [g_psA[b], g_es, g_sS], [g_sS])
        P.add("dve", lambda e: e.reciprocal(sS, sS), [g_sS], [g_sS])
        tt(pSb, pS, sS, ALU.mult, [g_pS, g_sS], [g_pSb])
        b = 5
        pv = pSb.rearrange("p (s h) -> p s h", s=16)
        for s_ in range(16):
            for g in range(4):
                for hh in range(2):
                    c = (2 * g) * 16 + s_
                    mm(psA[hh * 64:(hh + 1) * 64, b, c:c + 17:16], Vnb[:, s_, g * 64:(g + 1) * 64],
                       pv[:, s_, 4 * g + hh:4 * g + hh + 3:2], True, True, [g_Vnb, g_pSb], [g_psA[b]])
        cp(hT[:, :, SEQ:T], psA[:, b, 0:128].rearrange("p (k t) -> p k t", k=8), [g_psA[b]],
           [g_hT[k][16] for k in range(8)], eng="act")

        if dbg is not None and fl.get("dbg") == "aT":
            dma(dbg[:, 0:2048].bitcast(BF16).rearrange("p (k t) -> p k t", k=8), hT[:, :, 0:512],
                hT_all(0, 512), [], is_out=True)
        l_smp = P.stop()
        nd = 0
        while nd < len(l_smp) and l_smp[nd][4]:
            nd += 1
        k_ = (len(l_proj) * 2) // 5
        P.replay(l_smp[:nd])
        P.replay(l_proj[:k_])
        P.replay(l_proj[k_:], l_smp[nd:])
        state["nb"] = 6
        for si in range(2):
            proj_fm(bw[si][0], bw[si][1], range(4), [4], add_into_x(si * 4))

    if fl["kv"]:
        kv_attn()

    FO = 1024
    zst = [tmpA[:, FO + 1024 * i:FO + 1024 * (i + 1)].rearrange("p (k t) -> p k t", k=8) for i in range(2)]
    g_zst = [[G() for _ in range(8)] for _ in range(2)]
    yst = [tmpA[:, FO + 2048 + 1024 * i:FO + 2048 + 1024 * (i + 1)] for i in range(2)]
    g_yst = [G(), G()]
    zs = tmpA[:, FO + 4096:FO + 4096 + 128].rearrange("p (k t) -> p k t", k=8)
    g_zs = G()
    yss = tmpA[0:16, FO + 4224:FO + 4224 + 1024]
    g_yss = G()

    def final_norm(tiles):
        for ti in tiles:
            t0, n = TILES[ti]
            b = nextA()
            for k in range(8):
                i = k % 2
                act(nrm_sq[:, i, 0:n], xT[:, k, t0:t0 + n], AF.Square, [g_xT[k][ti]], [g_nsq[i]])
                mm(psA[:, b, 0:n], onesb[:], nrm_sq[:, i, 0:n], k == 0, k == 7, [g_nsq[i], g_misc], [g_psA[b]])
            r, g_r = rstd_from_ps(psA[:, b, 0:n], n, 1.0 / 1024.0, g_psA[b])
            if ti == 4:
                for k in range(8):
                    stt(zs[:, k, :], xT[:, k, t0:t0 + n], vcol[:, GFIN + k:GFIN + k + 1], r, ALU.mult, ALU.mult,
                        [g_xT[k][ti], g_r, g_vcol], [g_zs])
                b0 = nextA()
                b1 = nextA()
                for kk_ in range(8):
                    bb = b0 if kk_ < 4 else b1
                    tr(psA[0:16, bb, (kk_ % 4) * 128:(kk_ % 4 + 1) * 128], zs[:, kk_, :], ident,
                       [g_zs, g_cf], [g_psA[bb]])
                cp(yss[:, 0:512], psA[0:16, b0, :], [g_psA[b0]], [g_yss], eng="act")
                cp(yss[:, 512:1024], psA[0:16, b1, :], [g_psA[b1]], [g_yss], eng="act")
                dma(y_s, yss, [g_yss], [], is_out=True)
                continue
            for sub in range(4):
                j = ti * 4 + sub
                s = j % 2
                for k in range(8):
                    stt(zst[s][:, k, :], xT[:, k, t0 + sub * 128:t0 + (sub + 1) * 128],
                        vcol[:, GFIN + k:GFIN + k + 1], r[:, sub * 128:(sub + 1) * 128], ALU.mult, ALU.mult,
                        [g_xT[k][ti], g_r, g_vcol], [g_zst[s][k]])
                b0 = nextA()
                b1 = nextA()
                for kk_ in range(8):
                    bb = b0 if kk_ < 4 else b1
                    tr(psA[:, bb, (kk_ % 4) * 128:(kk_ % 4 + 1) * 128], zst[s][:, kk_, :], ident,
                       [g_zst[s][kk_], g_cf], [g_psA[bb]])
                cp(yst[s][:, 0:512], psA[:, b0, :], [g_psA[b0]], [g_yst[s]], eng="act")
                cp(yst[s][:, 512:1024], psA[:, b1, :], [g_psA[b1]], [g_yst[s]], eng="act")
                dma(y_p[j * 128:(j + 1) * 128, :], yst[s], [g_yst[s]], [g_yst[s]], is_out=True)

    if fl["mlp1"]:
        mlp(1, plan["mlp1"], tail=lambda: final_norm([0, 1]))
        final_norm([2, 3, 4])
    else:
        P.barrier()
        final_norm(range(5))

    with nc.Block() as block:
        P.emit(nc, block, ES)
    ES.close()
    return nc


_CACHE = {}


def kernel(x_prompt, x_sample, state_hgrn, cache_k_win, cache_v_win, w_a_in, a_lb, a_gnorm, w_a_out,
           g_mix, g_mlp, g_kv, w_kv, w_b_q, b_sink, w_b_out, rel_bias, w_up, w_down, g_final, _flags=None):
    f32 = lambda a: np.ascontiguousarray(np.asarray(a, dtype=np.float32))
    key = repr(sorted((_flags or {}).items()))
    if key not in _CACHE:
        _CACHE[key] = build(_flags)
    nc = _CACHE[key]
    cfc, t5c = make_consts()
    shared = {
        "w_a_in": f32(w_a_in)[0], "a_lb": f32(a_lb), "a_gnorm": f32(a_gnorm), "w_a_out": f32(w_a_out)[0],
        "g_mix": f32(g_mix), "g_mlp": f32(g_mlp), "g_kv": f32(g_kv).reshape(1, 1024), "w_kv": f32(w_kv),
        "w_b_q": f32(w_b_q)[0], "b_sink": f32(b_sink), "w_b_out": f32(w_b_out)[0], "rel_bias": f32(rel_bias),
        "w_up": f32(w_up), "w_down": f32(w_down), "g_final": f32(g_final).reshape(1, 1024),
        "cf_d": cfc, "t5_d": t5c,
    }
    xpn, xsn = f32(x_prompt), f32(x_sample)
    stn, ckn, cvn = f32(state_hgrn), f32(cache_k_win), f32(cache_v_win)
    in_maps = []
    for c in range(NCORES):
        m = dict(shared)
        m["xp"] = xpn[c]
        m["xs"] = xsn[c * NS:(c + 1) * NS, 0, :]
        m["st"] = stn[0, c * NS:(c + 1) * NS]
        m["ck"] = ckn[c * NS:(c + 1) * NS].reshape(NS, 128, 256)
        m["cv"] = cvn[c * NS:(c + 1) * NS].reshape(NS, 128, 256)
        in_maps.append(m)
    if _flags and _flags.get("trace"):
        res = run_bass_kernel_spmd(nc, in_maps, core_ids=list(range(NCORES)), trace=True)
        print("EXEC_TIME_NS", res.exec_time_ns)
    else:
        res = run_bass_kernel_spmd(nc, in_maps, core_ids=list(range(NCORES)))
    R = res.results
    y_prompt = np.stack([R[c]["y_p"] for c in range(NCORES)], 0)
    y_sample = np.concatenate([R[c]["y_s"] for c in range(NCORES)], 0).reshape(128, 1, 1024)
    s_prompt = np.stack([R[c]["st_p"] for c in range(NCORES)], 0)[None]
    s_sample = np.concatenate([R[c]["st_s"] for c in range(NCORES)], 0)[None]
    k_prompt = np.stack([R[c]["k_p"] for c in range(NCORES)], 0).reshape(8, 128, 4, 64)
    v_prompt = np.stack([R[c]["v_p"] for c in range(NCORES)], 0).reshape(8, 128, 4, 64)
    k_sample = np.concatenate([R[c]["k_s"] for c in range(NCORES)], 0).reshape(128, 128, 4, 64)
    v_sample = np.concatenate([R[c]["v_s"] for c in range(NCORES)], 0).reshape(128, 128, 4, 64)
    if _flags and _flags.get("dbg"):
        global DBG
        DBG = [R[c]["dbg"] for c in range(NCORES)]
    return (y_prompt, y_sample, s_prompt, s_sample, k_prompt, v_prompt, k_sample, v_sample)
```

```python
from contextlib import ExitStack
import math
import os
import numpy as np
import concourse.bass as bass
import concourse.mybir as mybir
from concourse.bass_utils import run_bass_kernel_spmd

F32 = mybir.dt.float32
BF16 = mybir.dt.bfloat16
AF = mybir.ActivationFunctionType
ALU = mybir.AluOpType

NCORES = 8
SEQ = 2048
NS = 16
T = SEQ + NS
TILES = [(0, 512), (512, 512), (1024, 512), (1536, 512), (2048, NS)]
EPS = 1e-6
NEG = -30000.0
ATTN_SCALE = 0.125
SKIP = set(os.environ.get('DEVSKIP', '').split(','))
SAME_ENG_FREE = os.environ.get('SAME_ENG_FREE', '0') == '1'

ENGS = ["pe", "act", "dve", "pool", "sp"]


class Gran:
    __slots__ = ("w", "r")

    def __init__(self):
        self.w = None
        self.r = []


class Op:
    __slots__ = ("eng", "fn", "deps", "dma", "sig", "val", "sem", "pos", "prev")


class Prog:
    def __init__(self):
        self.ops = []
        self.by_eng = {e: [] for e in ENGS}
        self.out_dmas = []

    def record(self):
        self.rec = []

    def stop(self):
        r, self.rec = self.rec, None
        return r

    def replay(self, *lists):
        idx = [0] * len(lists)
        while True:
            best, bf = None, None
            for li, l in enumerate(lists):
                if idx[li] < len(l):
                    f = idx[li] / float(len(l))
                    if best is None or f < bf:
                        best, bf = li, f
            if best is None:
                break
            self.add(*lists[best][idx[best]])
            idx[best] += 1

    def add(self, eng, fn, reads=(), writes=(), dma=False, out=False):
        if getattr(self, "rec", None) is not None:
            self.rec.append((eng, fn, list(reads), list(writes), dma, out))
            return None
        op = Op()
        op.eng, op.fn, op.dma, op.sig, op.val, op.sem, op.prev = eng, fn, dma, False, 0, None, None
        op.pos = len(self.ops)
        deps = set()
        for g in reads:
            if g.w is not None:
                deps.add(g.w)
        for g in writes:
            if g.w is not None:
                deps.add(g.w)
            deps.update(g.r)
        for g in writes:
            g.w = op
            g.r = []
        for g in reads:
            if g.w is not op:
                g.r.append(op)
        deps.discard(op)
        best = {}
        keep = set()
        for d in deps:
            if d.dma:
                keep.add(d)
            else:
                if d.eng == eng and not dma and (eng == "pe" or SAME_ENG_FREE):
                    continue
                b = best.get(d.eng)
                if b is None or d.pos > b.pos:
                    best[d.eng] = d
        keep.update(best.values())
        op.deps = keep
        for d in keep:
            d.sig = True
        self.ops.append(op)
        self.by_eng[eng].append(op)
        if out:
            self.out_dmas.append(op)
        return op

    def barrier(self, engs=("pe", "act", "dve", "pool")):
        lasts = []
        for e in engs:
            for o in reversed(self.by_eng[e]):
                if o.fn is not None and not o.dma:
                    lasts.append(o)
                    break
        start = getattr(self, "bar_pos", 0)
        dmas = [o for o in self.ops[start:] if o.dma and o.eng in ("sp", "act")]
        self.bar_pos = len(self.ops)
        for e in engs:
            op = Op()
            op.eng, op.fn, op.dma, op.sig, op.val, op.sem, op.prev = e, None, False, False, 0, None, None
            op.pos = len(self.ops)
            op.deps = set(l for l in lasts if l.eng != e) | set(dmas)
            for d in op.deps:
                d.sig = True
            self.ops.append(op)
            self.by_eng[e].append(op)

    def emit(self, nc, block, ES):
        ndma = {"sp": 24, "pool": 12, "act": 4}
        sems = []

        def newsem(name):
            s = ES.enter_context(nc.semaphore(name))
            sems.append(s)
            return len(sems) - 1

        eng_sem = {e: newsem("s_" + e) for e in ENGS}
        dma_sem = {q: [newsem("d_%s%d" % (q, i)) for i in range(n)] for q, n in ndma.items()}
        dma_cnt = {q: 0 for q in ndma}
        dma_last = {q: [None] * n for q, n in ndma.items()}
        dma_use = {q: [0] * n for q, n in ndma.items()}
        cnt = {e: 0 for e in ENGS}
        for op in self.ops:
            if op.dma:
                q = op.eng
                slot = dma_cnt[q] % ndma[q]
                dma_cnt[q] += 1
                op.sem = dma_sem[q][slot]
                dma_use[q][slot] += 1
                op.val = 16 * dma_use[q][slot]
                op.prev = dma_last[q][slot]
                dma_last[q][slot] = op
            elif op.sig:
                cnt[op.eng] += 1
                op.val = cnt[op.eng]
                op.sem = eng_sem[op.eng]
        handles = {"pe": "tensor", "act": "scalar", "dve": "vector", "pool": "gpsimd", "sp": "sync"}

        def run(ename, eng):
            seen = {}
            ops = self.by_eng[ename]
            for op in ops:
                waits = {}
                ds = list(op.deps)
                if op.prev is not None:
                    ds.append(op.prev)
                for d in ds:
                    if waits.get(d.sem, 0) < d.val:
                        waits[d.sem] = d.val
                for s, v in sorted(waits.items()):
                    if seen.get(s, 0) < v:
                        eng.wait_ge(sems[s], v)
                        seen[s] = v
                if op.fn is None:
                    continue
                ins = op.fn(eng)
                if op.dma:
                    ins.then_inc(sems[op.sem], 16)
                elif op.sig:
                    ins.then_inc(sems[op.sem], 1)
            if ename == "sp":
                for op in self.out_dmas:
                    if seen.get(op.sem, 0) < op.val:
                        eng.wait_ge(sems[op.sem], op.val)
                        seen[op.sem] = op.val

        for ename in ENGS:
            getattr(block, handles[ename])(lambda eng, ename=ename: run(ename, eng))


def _t5_bucket(dist):
    n = np.maximum(dist, 0)
    nf = np.maximum(n, 1).astype(np.float32)
    large = 16 + (np.log(nf / np.float32(16)) / np.float32(math.log(128 / 16)) * np.float32(16)).astype(np.int32)
    large = np.minimum(large, 31)
    return np.where(n < 16, n, large)


NCF = 128 + 128 + 512 + 2 + 128


def make_consts():
    cf = np.zeros((128, NCF), np.float32)
    cf[:, 0:128] = np.eye(128, dtype=np.float32)
    s = np.arange(128)[:, None]
    t = np.arange(128)[None, :]
    cf[:, 128:256] = ((s // 64 == t // 64) & (t >= s)).astype(np.float32)
    cf[:, 256:768] = (np.arange(512) % 64 != 0).astype(np.float32)[None, :]
    cf[:64, 768] = 1.0
    cf[64:, 769] = 1.0
    cf[:, 770:898] = np.eye(128, dtype=np.float32)[::-1]
    oh = np.zeros((32, 512), np.float32)
    mrow = np.full((16, 512), NEG, np.float32)
    for kb in range(2):
        for delta in range(-127, 128):
            dist = delta if kb == 1 else delta + 128
            if 0 <= dist < 128:
                b = int(_t5_bucket(np.array([dist]))[0])
                oh[b, kb * 256 + delta + 127] = 1.0
                mrow[:, kb * 256 + delta + 127] = 0.0
    t5 = np.zeros((32, 1024), np.float32)
    t5[:, 0:512] = oh
    t5[0:16, 512:1024] = mrow
    return cf, t5


def build(flags=None):
    fl = dict(hgrn=True, mlp0=True, kv=True, attn=True, mlp1=True, sample=True)
    if flags:
        fl.update(flags)
    nc = bass.Bass("TRN2", target_bir_lowering=False)
    P = Prog()
    ES = ExitStack()

    def din(name, shape):
        return nc.dram_tensor(name, shape, F32, kind="ExternalInput").ap()

    def dout(name, shape):
        return nc.dram_tensor(name, shape, F32, kind="ExternalOutput").ap()

    xp = din("xp", [SEQ, 1024])
    xs = din("xs", [NS, 1024])
    st_in = din("st", [NS, 8, 128, 128])
    ck = din("ck", [NS, 128, 256])
    cv = din("cv", [NS, 128, 256])
    w_a_in = din("w_a_in", [1024, 4096])
    a_lb = din("a_lb", [2, 1024])
    a_gnorm = din("a_gnorm", [1, 128])
    w_a_out = din("w_a_out", [1024, 1024])
    g_mix = din("g_mix", [2, 1024])
    g_mlp = din("g_mlp", [2, 1024])
    g_kv = din("g_kv", [1, 1024])
    w_kv = din("w_kv", [1024, 512])
    w_b_q = din("w_b_q", [1024, 1024])
    b_sink = din("b_sink", [1, 16])
    w_b_out = din("w_b_out", [1024, 1024])
    rel_bias = din("rel_bias", [32, 16])
    w_up = din("w_up", [2, 1024, 4096])
    w_down = din("w_down", [2, 4096, 1024])
    g_final = din("g_final", [1, 1024])
    cfd = din("cf_d", [128, NCF])
    t5d = din("t5_d", [32, 1024])

    y_p = dout("y_p", [SEQ, 1024])
    y_s = dout("y_s", [NS, 1024])
    st_p = dout("st_p", [8, 128, 128])
    st_s = dout("st_s", [NS, 8, 128, 128])
    k_p = dout("k_p", [128, 256])
    v_p = dout("v_p", [128, 256])
    k_s = dout("k_s", [NS, 128, 256])
    v_s = dout("v_s", [NS, 128, 256])
    tabD = nc.dram_tensor("tabD", [16, 512], F32, kind="Internal").ap()
    dbg = dout("dbg", [128, 8192]) if fl.get("dbg") else None

    def sb(name, shape, dt):
        return ES.enter_context(nc.sbuf_tensor(name, shape, dt))

    def G():
        return Gran()

    xT = sb("xT", [128, 8, T], F32)
    hT = sb("hT", [128, 8, T], BF16)
    arena = sb("arena", [128, 16 * 1040], BF16)
    TMPA_BYTES = 41 * 1024
    tmpA = sb("tmpA", [128, TMPA_BYTES // 4], F32)
    cf = sb("cf", [128, NCF], F32)
    identb = sb("identb", [128, 128], BF16)
    onesb = sb("onesb", [128, 128], BF16)
    vcol = sb("vcol", [128, 72], F32)
    lbv = sb("lbv", [128, 3, 8], F32)
    esink = sb("esink", [128, 16], F32)
    kvs_full = sb("kvs", [128, 512], F32)
    kvs = kvs_full[0:16, :]
    atok = sb("atok", [128, 1024], BF16)
    NSLOT = 2
    wring = [sb("wr%d" % i, [128, 8 * 512], BF16) for i in range(NSLOT)]
    g_wring = [G() for _ in range(NSLOT)]
    psA = ES.enter_context(nc.psum_tensor("psA", [128, 6, 512], F32))
    psB = ES.enter_context(nc.psum_tensor("psB", [128, 2, 1024], BF16))
    g_psA = [G() for _ in range(6)]
    g_psB = [G() for _ in range(2)]
    g_xT = [[G() for _ in TILES] for _ in range(8)]
    g_hT = [[G() for _ in range(17)] for _ in range(8)]
    g_cf = G()
    g_misc = G()

    ident = cf[:, 0:128]
    hmask = cf[:, 128:256]
    scanmask = cf[:, 256:768]
    mlo = cf[:, 768:769]
    mhi = cf[:, 769:770]

    state = {"bankA": 0, "bankB": 0, "nb": 6, "base": 0}
    phcell = sb("phcell", [128, 2], F32)
    g_phase = G()

    def phase_barrier():
        P.barrier()
        P.add("dve", lambda e: e.memset(phcell[:], 0.0), (), [g_phase])

    def nextA():
        b = state["bankA"] % state["nb"]
        state["bankA"] = (b + 1) % state["nb"]
        return b + state["base"]

    def nextB():
        b = state["bankB"]
        state["bankB"] = (b + 1) % 2
        return b

    def hT_gr(k, t0, n):
        if t0 >= SEQ:
            return [g_hT[k][16]]
        r = [g_hT[k][j] for j in range(t0 // 128, (min(t0 + n, SEQ) + 127) // 128)]
        if t0 + n > SEQ:
            r.append(g_hT[k][16])
        return r

    def hT_all(t0, n):
        r = []
        for k in range(8):
            r += hT_gr(k, t0, n)
        return r

    def mm(out, lhsT, rhs, start, stop, reads, writes):
        P.add("pe", lambda e: e.matmul(out, lhsT, rhs, start=start, stop=stop), reads, writes)

    def tr(out, in_, idn, reads, writes):
        P.add("pe", lambda e: e.transpose(out, in_, idn), reads, writes)

    def act(out, in_, func, reads, writes, scale=None, bias=None):
        kw = {}
        if scale is not None:
            kw["scale"] = scale
        if bias is not None:
            kw["bias"] = bias
        P.add("act", lambda e: e.activation(out, in_, func, **kw), reads, writes)

    def tt(out, in0, in1, op, reads, writes, eng="dve"):
        P.add(eng, lambda e: e.tensor_tensor(out, in0, in1, op), reads, writes)

    def ts(out, in0, s1, s2, op0, op1, reads, writes, eng="dve"):
        if op1 is None:
            P.add(eng, lambda e: e.tensor_scalar(out, in0, s1, None, op0), reads, writes)
        else:
            P.add(eng, lambda e: e.tensor_scalar(out, in0, s1, s2, op0, op1), reads, writes)

    def stt(out, in0, scalar, in1, op0, op1, reads, writes):
        P.add("dve", lambda e: e.scalar_tensor_tensor(out, in0, scalar, in1, op0, op1), reads, writes)

    def cp(out, in_, reads, writes, eng="dve"):
        if eng == "act":
            P.add(eng, lambda e: e.activation(out, in_, AF.Copy), reads, writes)
        else:
            P.add(eng, lambda e: e.tensor_copy(out, in_), reads, writes)

    def mset(ap, val, writes, eng="dve"):
        P.add(eng, lambda e: e.memset(ap, val), (), writes)

    def dma(out, in_, reads, writes, q="sp", is_out=False, **kw):
        return P.add(q, lambda e: e.dma_start(out=out, in_=in_, **kw), reads, writes, dma=True, out=is_out)

    wspecs = []

    def wsrc(kind, a):
        if kind == "ain":
            h = a
            src = w_a_in.rearrange("(k p) (s c) -> p k s c", p=128, s=4)[:, :, :, h * 128:(h + 1) * 128]
            return src, lambda w: w[:, 0:4096].rearrange("p (k s c) -> p k s c", k=8, s=4)
        if kind == "k8":
            mat, c0 = a
            src = mat.rearrange("(k p) c -> p k c", p=128)[:, :, c0:c0 + 512]
            return src, lambda w: w[:, 0:4096].rearrange("p (k c) -> p k c", k=8)
        if kind == "down":
            l, kg, cq = a
            src = w_down[l].rearrange("(k p) c -> p k c", p=128)[:, kg * 16:(kg + 1) * 16, cq * 256:(cq + 1) * 256]
            return src, lambda w: w[:, 0:4096].rearrange("p (k c) -> p k c", k=16)
        raise ValueError(kind)

    wstate = {"issued": 0}

    def w_plan(kind, a):
        wspecs.append((kind, a))
        return len(wspecs) - 1

    def w_use(i, limit=None):
        lim = i + NSLOT - 1 if limit is None else limit
        while wstate["issued"] < len(wspecs) and wstate["issued"] <= lim:
            j = wstate["issued"]
            kind, a = wspecs[j]
            src, view = wsrc(kind, a)
            slot = j % NSLOT
            if kind == "ain":
                for sec in range(4):
                    dma(view(wring[slot])[:, :, sec, :], src[:, :, sec, :], (), [g_wring[slot]], q="pool")
            else:
                dma(view(wring[slot]), src, (), [g_wring[slot]], q="pool")
            wstate["issued"] += 1
        kind, a = wspecs[i]
        _, view = wsrc(kind, a)
        return view(wring[i % NSLOT]), g_wring[i % NSLOT]

    plan = {}
    if fl["hgrn"]:
        plan["ain"] = [w_plan("ain", h) for h in range(8)]
        plan["aout"] = [w_plan("k8", (w_a_out, c)) for c in (0, 512)]

    def plan_mlp(l):
        r = []
        for half in range(2):
            for ffh in range(2):
                ups = [w_plan("k8", (w_up[l], ffh * 2048 + j * 512)) for j in range(4)]
                downs = [w_plan("down", (l, ffh, cq)) for cq in range(4)]
                r.append((ups, downs))
        return r

    if fl["mlp0"]:
        plan["mlp0"] = plan_mlp(0)
    if fl["kv"]:
        plan["kv"] = w_plan("k8", (w_kv, 0))
    if fl["attn"]:
        plan["q"] = [w_plan("k8", (w_b_q, c)) for c in (0, 512)]
        plan["bout"] = [w_plan("k8", (w_b_out, c)) for c in (0, 512)]
    if fl["mlp1"]:
        plan["mlp1"] = plan_mlp(1)

    dma(cf[:], cfd, (), [g_cf])
    cp(identb[:], ident, [g_cf], [g_misc], eng="act")
    mset(onesb[:], 1.0, [g_misc])
    vst = tmpA[0:72, 0:128]
    g_vst = G()
    mset(tmpA[0:72, 0:128], 0.0, [g_vst])
    dma(vst[0:16, :], a_lb.rearrange("r (k p) -> (r k) p", p=128), (), [g_vst])
    dma(vst[16:32, :], g_mix.rearrange("r (k p) -> (r k) p", p=128), (), [g_vst])
    dma(vst[32:48, :], g_mlp.rearrange("r (k p) -> (r k) p", p=128), (), [g_vst])
    dma(vst[48:56, :], g_kv.rearrange("r (k p) -> (r k) p", p=128), (), [g_vst])
    dma(vst[56:64, :], g_final.rearrange("r (k p) -> (r k) p", p=128), (), [g_vst])
    dma(vst[64:65, :], a_gnorm, (), [g_vst])
    b = nextA()
    tr(psA[:, b, 0:72], vst, cf[0:72, 0:72], [g_vst, g_cf], [g_psA[b]])
    g_vcol = G()
    cp(vcol[:], psA[:, b, 0:72], [g_psA[b]], [g_vcol])
    tt(lbv[:, 1, :], vcol[:, 0:8], vcol[:, 8:16], ALU.subtract, [g_vcol], [g_vcol])
    act(lbv[:, 0, :], lbv[:, 1, :], AF.Sigmoid, [g_vcol], [g_vcol])
    ts(lbv[:, 1, :], lbv[:, 0, :], -1.0, 1.0, ALU.mult, ALU.add, [g_vcol], [g_vcol])
    ts(lbv[:, 2, :], lbv[:, 0, :], -1.0, None, ALU.add, None, [g_vcol], [g_vcol])
    GMIX, GMLP, GKV, GFIN, GNORM = 16, 32, 48, 56, 64

    xst = [tmpA[:, 1024 * i + 256:1024 * (i + 1) + 256] for i in range(4)]
    g_xst = [G(), G(), G(), G()]
    for j in range(16):
        s = j % 4
        dma(xst[s], xp[j * 128:(j + 1) * 128, :], (), [g_xst[s]])
        for hk in range(2):
            b = nextA()
            for kk_ in range(4):
                k = hk * 4 + kk_
                tr(psA[:, b, kk_ * 128:(kk_ + 1) * 128], xst[s][:, k * 128:(k + 1) * 128], ident,
                   [g_xst[s], g_cf], [g_psA[b]])
            eng = "act" if hk == 0 else "dve"
            cp(xT[:, hk * 4:hk * 4 + 4, j * 128:(j + 1) * 128],
               psA[:, b, :].rearrange("p (k t) -> p k t", k=4),
               [g_psA[b]], [g_xT[hk * 4 + i][j // 4] for i in range(4)], eng=eng)
    xss = tmpA[0:16, 4352:4352 + 1024]
    g_xss = G()
    dma(xss, xs, (), [g_xss])
    b = nextA()
    for k in range(8):
        tr(psA[:, b, k * 16:(k + 1) * 16], xss[:, k * 128:(k + 1) * 128], cf[0:16, 0:16], [g_xss, g_cf], [g_psA[b]])
    cp(xT[:, :, SEQ:T], psA[:, b, 0:128].rearrange("p (k t) -> p k t", k=8), [g_psA[b]],
       [g_xT[k][4] for k in range(8)])

    nrm_sq = sb("nrm_sq", [128, 2, 512], BF16)
    nrm_r = sb("nrm_r", [128, 2, 512], F32)
    g_nsq = [G(), G()]
    g_nr = [G(), G()]
    nst = {"i": 0}

    def rstd_from_ps(ps_ap, n, inv_d, g_ps):
        i = nst["i"] % 2
        nst["i"] += 1
        r = nrm_r[:, i, 0:n]
        act(r, ps_ap, AF.Ln, [g_ps], [g_nr[i]], scale=inv_d, bias=EPS)
        act(r, r, AF.Exp, [g_nr[i]], [g_nr[i]], scale=-0.5)
        return r, g_nr[i]

    def rmsnorm(gbase, tiles=range(5), out_fn=None):
        for ti in tiles:
            t0, n = TILES[ti]
            b = nextA()
            for k in range(8):
                i = k % 2
                act(nrm_sq[:, i, 0:n], xT[:, k, t0:t0 + n], AF.Square, [g_xT[k][ti]], [g_nsq[i]])
                mm(psA[:, b, 0:n], onesb[:], nrm_sq[:, i, 0:n], k == 0, k == 7, [g_nsq[i], g_misc], [g_psA[b]])
            r, g_r = rstd_from_ps(psA[:, b, 0:n], n, 1.0 / 1024.0, g_psA[b])
            for k in range(8):
                if out_fn is None:
                    stt(hT[:, k, t0:t0 + n], xT[:, k, t0:t0 + n], vcol[:, gbase + k:gbase + k + 1], r,
                        ALU.mult, ALU.mult, [g_xT[k][ti], g_r, g_vcol], hT_gr(k, t0, n))
                else:
                    out_fn(ti, k, r, g_r)

    def proj_fm(wv, g_w, mlist, tiles, consume, src=None, src_gr=None):
        if src is None:
            src, src_gr = hT, hT_gr
        for m in mlist:
            for ti in tiles:
                t0, n = TILES[ti]
                b = nextA()
                for k in range(8):
                    mm(psA[:, b, 0:n], wv[:, k, m * 128:(m + 1) * 128], src[:, k, t0:t0 + n], k == 0, k == 7,
                       [g_w] + src_gr(k, t0, n), [g_psA[b]])
                consume(m, ti, b)

    def add_into_x(mg):
        def f(m, ti, b):
            t0, n = TILES[ti]
            k = mg + m
            tt(xT[:, k, t0:t0 + n], xT[:, k, t0:t0 + n], psA[:, b, 0:n], ALU.add,
               [g_psA[b], g_xT[k][ti]], [g_xT[k][ti]])
        return f


    smp = sb("smp", [128, 5, 8, NS], F32)
    qsb = sb("qsb", [128, 8, NS], BF16)
    decs = sb("decs", [128, 4, 8], F32)
    stcur = sb("stcur", [128, 128], F32)
    cks = sb("cks", [128, 3, 8], F32)
    sgs = sb("sgs", [128, NS], F32)
    vsD = nc.dram_tensor("vsD", [NS, 1024], F32, kind="Internal").ap()

    def hgrn():
        P.barrier()
        rmsnorm(GMIX)
        yT = arena[:, 0:8 * T].rearrange("p (k t) -> p k t", k=8)
        g_y = [[G() for _ in range(5)] for _ in range(8)]
        off = [0]

        def carve(nf32, dt=F32):
            a = tmpA[:, off[0]:off[0] + nf32]
            off[0] += nf32
            return a.bitcast(BF16) if dt == BF16 else a

        names = ["ktil", "kstT", "qtil", "qin", "gate", "vtok"]
        sets = [{nm: carve(256, BF16) for nm in names} for _ in range(2)]
        gs = [{nm: G() for nm in names} for _ in range(2)]
        kvsb = kvs_full[:, :].bitcast(BF16)
        sets.append({"ktil": atok[:, 0:512], "kstT": atok[:, 512:1024], "qtil": kvsb[:, 0:512], "qin": kvsb[:, 512:1024],
                     "gate": nrm_sq[:, 0, :], "vtok": nrm_sq[:, 1, :]})
        gs.append({"ktil": G(), "kstT": G(), "qtil": G(), "qin": G(), "gate": g_nsq[0], "vtok": g_nsq[1]})
        tA, tB, tC, tD, tE, tE2 = [carve(512) for _ in range(6)]
        g_tA, g_tB, g_tC, g_tD, g_tE, g_tE2 = [G() for _ in range(6)]
        kst_lo, kst_hi = carve(256, BF16), carve(256, BF16)
        g_klo, g_khi = G(), G()
        U_sb, Dbc = carve(1024), carve(1024)
        g_U, g_Dbc = G(), G()
        Sbf = [carve(512, BF16), carve(512, BF16)]
        g_Sbf = [G(), G()]
        attT, sqb, grs = carve(256, BF16), carve(256, BF16), carve(256, BF16)
        g_attT, g_sqb, g_grs = G(), G(), G()
        assert off[0] <= TMPA_BYTES // 4
        g_dec = [G(), G()]
        g_dec0 = [G(), G()]
        g_cks = G()
        g_stcur = G()
        g_smp, g_qsb, g_sgs = G(), G(), G()
        mset(decs[:, 2:4, 0:1], 0.0, [g_dec0[0], g_dec0[1]])
        U3 = U_sb.rearrange("p (v c) -> p v c", c=8)
        D3 = Dbc.rearrange("p (v c) -> p v c", c=8)

        def col(i, h):
            return lbv[:, i, h:h + 1]

        def stageA(h, ti, S, D):
            t0 = ti * 512
            wv, g_w = w_use(plan["ain"][h])
            st, g = sets[S], gs[S]
            bf = 0
            for k in range(8):
                mm(psA[:, bf, :], wv[:, k, 1, :], hT[:, k, t0:t0 + 512], k == 0, k == 7, [g_w] + hT_gr(k, t0, 512), [g_psA[bf]])
            act(tA, psA[:, bf, :], AF.Sigmoid, [g_psA[bf]], [g_tA])
            bg = 1
            for k in range(8):
                mm(psA[:, bg, :], wv[:, k, 3, :], hT[:, k, t0:t0 + 512], k == 0, k == 7, [g_w] + hT_gr(k, t0, 512), [g_psA[bg]])
            act(tE2, psA[:, bg, :], AF.Sigmoid, [g_psA[bg]], [g_tE2])
            tt(st["gate"], psA[:, bg, :], tE2, ALU.mult, [g_psA[bg], g_tE2], [g["gate"]])
            act(tB, tA, AF.Ln, [g_tA, g_vcol], [g_tB], scale=col(1, h), bias=col(0, h))
            act(tD, tA, AF.Identity, [g_tA, g_vcol], [g_tD], scale=col(2, h), bias=col(1, h))
            P.add("dve", lambda e: e.tensor_tensor_scan(tC, scanmask, tB, 0.0, ALU.mult, ALU.add), [g_tB, g_cf], [g_tC])
            tC3 = tC.rearrange("p (c s) -> p c s", s=64)
            tt(tA.rearrange("p (c s) -> p c s", s=64), tC3, tC3[:, :, 31:32].broadcast_to([128, 8, 64]), ALU.subtract, [g_tC], [g_tA])
            tt(cks[:, 2, :], tC[:, 63:512:64], tC[:, 31:512:64], ALU.subtract, [g_tC], [g_cks])
            act(cks[:, 0, :], cks[:, 2, :], AF.Exp, [g_cks], [g_cks])
            act(cks[:, 1, :], tC[:, 31:512:64], AF.Exp, [g_tC], [g_cks])
            act(decs[:, D, :], tC[:, 63:512:64], AF.Exp, [g_tC], [g_dec[D]])
            act(tE, tA, AF.Exp, [g_tA], [g_tE], scale=-1.0)
            tt(st["ktil"], tD, tE, ALU.mult, [g_tD, g_tE], [g["ktil"]])
            tt(st["kstT"].rearrange("p (c s) -> p c s", s=64), st["ktil"].rearrange("p (c s) -> p c s", s=64),
               cks[:, 0, :].unsqueeze(2).broadcast_to([128, 8, 64]), ALU.mult, [g["ktil"], g_cks], [g["kstT"]])
            act(tA, tA, AF.Exp, [g_tA], [g_tA])
            bq = 0
            for k in range(8):
                mm(psA[:, bq, :], wv[:, k, 0, :], hT[:, k, t0:t0 + 512], k == 0, k == 7, [g_w] + hT_gr(k, t0, 512), [g_psA[bq]])
            tt(st["qtil"], psA[:, bq, :], tA, ALU.mult, [g_psA[bq], g_tA], [g["qtil"]])
            tt(st["qin"].rearrange("p (c s) -> p c s", s=64), st["qtil"].rearrange("p (c s) -> p c s", s=64),
               cks[:, 1, :].unsqueeze(2).broadcast_to([128, 8, 64]), ALU.mult, [g["qtil"], g_cks], [g["qin"]])
            bv = 1
            for sub in range(4):
                for k in range(8):
                    mm(psA[:, bv, sub * 128:(sub + 1) * 128], hT[:, k, t0 + sub * 128:t0 + (sub + 1) * 128], wv[:, k, 2, :],
                       k == 0, k == 7, [g_w, g_hT[k][ti * 4 + sub]], [g_psA[bv]])
            cp(st["vtok"], psA[:, bv, :], [g_psA[bv]], [g["vtok"]], eng="act")

        def stageS(h):
            wv, g_w = w_use(plan["ain"][h])
            for sec in (1, 0, 3, 2):
                b = {1: 0, 0: 1, 3: 0, 2: 1}[sec]
                for k in range(8):
                    mm(psA[:, b, 0:NS], wv[:, k, sec, :], hT[:, k, SEQ:T], k == 0, k == 7, [g_w, g_hT[k][16]], [g_psA[b]])
                pv = psA[:, b, 0:NS]
                if sec == 1:
                    act(sgs[:], pv, AF.Sigmoid, [g_psA[b]], [g_sgs])
                    act(smp[:, 0, h, :], sgs[:], AF.Identity, [g_sgs, g_vcol], [g_smp], scale=col(1, h), bias=col(0, h))
                    act(smp[:, 1, h, :], sgs[:], AF.Identity, [g_sgs, g_vcol], [g_smp], scale=col(2, h), bias=col(1, h))
                elif sec == 0:
                    cp(qsb[:, h, :], pv, [g_psA[b]], [g_qsb], eng="dve")
                elif sec == 3:
                    act(sgs[:], pv, AF.Sigmoid, [g_psA[b]], [g_sgs])
                    tt(smp[:, 2, h, :], pv, sgs[:], ALU.mult, [g_psA[b], g_sgs], [g_smp])
                else:
                    cp(smp[:, 3, h, :], pv, [g_psA[b]], [g_smp], eng="dve")

        def stageB(h, ti, S, D, SB):
            st, g = sets[S], gs[S]
            bB = nextB()
            for sub in range(4):
                tr(psB[:, bB, sub * 128:(sub + 1) * 128], st["kstT"][:, sub * 128:(sub + 1) * 128], identb[:],
                   [g["kstT"], g_misc], [g_psB[bB]])
            act(kst_lo, psB[:, bB, 0:512], AF.Copy, [g_psB[bB], g_cf], [g_klo], scale=mlo)
            act(kst_hi, psB[:, bB, 0:512], AF.Copy, [g_psB[bB], g_cf], [g_khi], scale=mhi)
            for hb in range(2):
                bu = 2 + hb
                for cc in range(4):
                    c = hb * 4 + cc
                    sub, half = c // 2, c % 2
                    ks, gk = (kst_lo, g_klo) if half == 0 else (kst_hi, g_khi)
                    mm(psA[:, bu, cc * 128:(cc + 1) * 128], ks[:, sub * 128:(sub + 1) * 128],
                       st["vtok"][:, sub * 128:(sub + 1) * 128], True, True, [gk, g["vtok"]], [g_psA[bu]])
                cp(U3[:, :, hb * 4:(hb + 1) * 4].transpose([0, 2, 1]), psA[:, bu, :].rearrange("p (c v) -> p c v", c=4),
                   [g_psA[bu]], [g_U], eng=("dve" if hb == 0 else "act"))
            if ti > 0:
                stt(U3[:, :, 0], stcur[:], decs[:, D, 0:1], U3[:, :, 0], ALU.mult, ALU.add, [g_stcur, g_dec[D], g_U], [g_U])
            cp(decs[:, 2 + D, 1:8], decs[:, D, 1:8], [g_dec[D]], [g_dec0[D]], eng="dve")
            act(D3, decs[:, 2 + D, :].unsqueeze(1).broadcast_to([128, 128, 8]), AF.Copy, [g_dec0[D]], [g_Dbc])
            P.add("dve", lambda e: e.tensor_tensor_scan(U_sb, Dbc, U_sb, 0.0, ALU.mult, ALU.add), [g_Dbc, g_U], [g_U])
            cp(stcur[:], U3[:, :, 7], [g_U], [g_stcur], eng="dve")
            cp(Sbf[SB].rearrange("p (c v) -> p c v", c=8), U3.transpose([0, 2, 1]), [g_U], [g_Sbf[SB]], eng="dve")
            if ti == 3:
                dma(st_p[h], stcur[:], [g_stcur], [g_stcur], is_out=True)

        def stageC(h, ti, S, SB):
            st, g = sets[S], gs[S]
            t0 = ti * 512
            ba = 4
            for sub in range(4):
                mm(psA[:, ba, sub * 128:(sub + 1) * 128], st["ktil"][:, sub * 128:(sub + 1) * 128],
                   st["qtil"][:, sub * 128:(sub + 1) * 128], True, True, [g["ktil"], g["qtil"]], [g_psA[ba]])
            tt(attT.rearrange("p (a t) -> p a t", a=4), psA[:, ba, :].rearrange("p (a t) -> p a t", a=4),
               hmask.unsqueeze(1).broadcast_to([128, 4, 128]), ALU.mult, [g_psA[ba], g_cf], [g_attT])
            bo = 5
            for sub in range(4):
                mm(psA[:, bo, sub * 128:(sub + 1) * 128], st["vtok"][:, sub * 128:(sub + 1) * 128],
                   attT[:, sub * 128:(sub + 1) * 128], True, False, [g["vtok"], g_attT], [g_psA[bo]])
                for half in range(2):
                    c = 2 * sub + half
                    if ti == 0 and c == 0:
                        continue
                    if c == 0:
                        lhs, gl = Sbf[1 - SB][:, 7 * 128:8 * 128], g_Sbf[1 - SB]
                    else:
                        lhs, gl = Sbf[SB][:, (c - 1) * 128:c * 128], g_Sbf[SB]
                    mm(psA[:, bo, c * 64:(c + 1) * 64], lhs, st["qin"][:, c * 64:(c + 1) * 64], False, half == 1,
                       [gl, g["qin"]], [g_psA[bo]])
            act(sqb, psA[:, bo, :], AF.Square, [g_psA[bo]], [g_sqb])
            bs = 4
            mm(psA[:, bs, :], onesb[:], sqb, True, True, [g_sqb, g_misc], [g_psA[bs]])
            r, g_r = rstd_from_ps(psA[:, bs, :], 512, 1.0 / 128.0, g_psA[bs])
            tt(grs, st["gate"], r, ALU.mult, [g["gate"], g_r], [g_grs])
            stt(yT[:, h, t0:t0 + 512], psA[:, bo, :], vcol[:, GNORM:GNORM + 1], grs, ALU.mult, ALU.mult,
                [g_psA[bo], g_grs, g_vcol], [g_y[h][ti]])

        its = [(h, ti) for h in range(8) for ti in range(4)]
        N_IT = len(its)
        for step in range(-2, N_IT):
            lists = []
            ia = step + 2
            P.record()
            if ia <= N_IT and fl["sample"]:
                if ia == N_IT:
                    stageS(7)
                elif its[ia][1] == 0 and its[ia][0] > 0:
                    stageS(its[ia][0] - 1)
            if ia < N_IT:
                stageA(its[ia][0], its[ia][1], ia % 3, ia % 2)
            la = P.stop()
            if la:
                lists.append(la)
            ib = step + 1
            if 0 <= ib < N_IT:
                P.record()
                stageB(its[ib][0], its[ib][1], ib % 3, ib % 2, ib % 2)
                lists.append(P.stop())
            if 0 <= step < N_IT:
                P.record()
                stageC(its[step][0], its[step][1], step % 3, step % 2)
                lists.append(P.stop())
            P.replay(*lists)

        phase_barrier()

        def y_gr(k, t0, n):
            return [g_y[k][t0 // 512]]

        i0_, i1_ = plan["aout"]
        aw = [w_use(i0_, limit=i1_), w_use(i1_, limit=i1_)]
        state["nb"] = 4
        state["bankA"] = 0
        P.record()
        for si in range(2):
            proj_fm(aw[si][0], aw[si][1], range(4), range(4), add_into_x(si * 4), src=yT, src_gr=y_gr)
        l_proj = P.stop()
        P.record()
        if fl["sample"]:
            stS = [tmpA[:, 2048 * i:2048 * (i + 1)].rearrange("p (s h v) -> p s h v", s=2, h=8) for i in range(3)]
            g_stS = [G(), G(), G()]
            vbc = [tmpA[:, 6144 + 1024 * i:6144 + 1024 * (i + 1)] for i in range(4)]
            g_vbc = [G(), G(), G(), G()]
            vtk = nrm_r[0:16, :, :].rearrange("p a b -> p (a b)")
            b0, b1 = 4, 5
            for hh_ in range(8):
                bb = b0 if hh_ < 4 else b1
                tr(psA[0:16, bb, (hh_ % 4) * 128:(hh_ % 4 + 1) * 128], smp[:, 3, hh_, :], ident, [g_smp, g_cf], [g_psA[bb]])
            cp(vtk[:, 0:512], psA[0:16, b0, :], [g_psA[b0]], [g_nr[0]], eng="dve")
            cp(vtk[:, 512:1024], psA[0:16, b1, :], [g_psA[b1]], [g_nr[1]], eng="dve")
            g_vsD = G()
            dma(vsD, vtk, [g_nr[0], g_nr[1]], [g_vsD])
            bo = 5
            def ld_state(sg):
                dma(stS[sg % 3], st_in[sg * 2:(sg + 1) * 2].rearrange("s h k v -> k s h v"), [g_phase], [g_stS[sg % 3]])

            def ld_vbc(s_):
                dma(vbc[s_ % 4], vsD[s_:s_ + 1, :].broadcast_to([128, 1024]), [g_vsD, g_phase], [g_vbc[s_ % 4]], q="act")
            ld_state(0)
            ld_state(1)
            for s_ in range(3):
                ld_vbc(s_)
            stbf3 = nrm_sq[:, :, :].rearrange("p a (h v) -> p (a h) v", v=128)
            def smm(s_):
                for h in range(8):
                    mm(psA[:, bo, h * 16 + s_:h * 16 + s_ + 1], stbf3[:, h, :], qsb[:, h, s_:s_ + 1], True, True,
                       [g_nsq[0], g_nsq[1], g_qsb], [g_psA[bo]])

            for sg in range(8):
                bi = sg % 3
                if sg + 2 < 8:
                    ld_state(sg + 2)
                for si_ in range(2):
                    s_ = sg * 2 + si_
                    vi = s_ % 4
                    if s_ + 3 < NS:
                        ld_vbc(s_ + 3)
                    vb3 = vbc[vi].rearrange("p (h v) -> p h v", h=8)
                    st3 = stS[bi][:, si_, :, :]
                    tt(vb3, vb3, smp[:, 1, :, s_].unsqueeze(2).broadcast_to([128, 8, 128]), ALU.mult,
                       [g_vbc[vi], g_smp], [g_vbc[vi]])
                    tt(st3, st3, smp[:, 0, :, s_].unsqueeze(2).broadcast_to([128, 8, 128]), ALU.mult,
                       [g_stS[bi], g_smp], [g_stS[bi]])
                    tt(st3, st3, vb3, ALU.add, [g_stS[bi], g_vbc[vi]], [g_stS[bi]])
                    if s_ > 0:
                        smm(s_ - 1)
                    act(stbf3, st3, AF.Copy, [g_stS[bi]], [g_nsq[0], g_nsq[1]])
                    if s_ == NS - 1:
                        smm(s_)
                dma(st_s[sg * 2:(sg + 1) * 2].rearrange("s h k v -> k s h v"), stS[bi], [g_stS[bi]], [g_stS[bi]], is_out=True)
            sq2 = sqb[:, 0:128]
            act(sq2, psA[:, bo, 0:128], AF.Square, [g_psA[bo]], [g_sqb])
            bs = 4
            mm(psA[:, bs, 0:128], onesb[:], sq2, True, True, [g_sqb, g_misc], [g_psA[bs]])
            r, g_r = rstd_from_ps(psA[:, bs, 0:128], 128, 1.0 / 128.0, g_psA[bs])
            g_gr2 = G()
            gr2 = tmpA[:, 10240:10240 + 128]
            tt(gr2, smp[:, 2, :, :].rearrange("p h s -> p (h s)"), r, ALU.mult, [g_smp, g_r], [g_gr2])
            stt(yT[:, :, SEQ:T], psA[:, bo, 0:128].rearrange("p (h s) -> p h s", h=8), vcol[:, GNORM:GNORM + 1],
                gr2.rearrange("p (h s) -> p h s", h=8), ALU.mult, ALU.mult, [g_psA[bo], g_gr2, g_vcol],
                [g_y[k][4] for k in range(8)])
        else:
            mset(yT[:, :, SEQ:T], 0.0, [g_y[k][4] for k in range(8)])
            mset(stcur[:], 0.0, [g_stcur])
            for sg in range(4):
                dma(st_s[sg * 4:(sg + 1) * 4].rearrange("s h k v -> k (s h) v"),
                    stcur[:].unsqueeze(1).broadcast_to([128, 32, 128]), [g_stcur], [], is_out=True)
        l_smp = P.stop()
        P.replay(l_proj, l_smp)
        state["nb"] = 6
        for si in range(2):
            proj_fm(aw[si][0], aw[si][1], range(4), [4], add_into_x(si * 4), src=yT, src_gr=y_gr)

    if fl["hgrn"]:
        hgrn()

    MT = [(0, 512), (512, 512), (1024, 512), (1536, 264), (1800, 264)]

    def xg(k, t0, n):
        r = []
        for ti, (a, w) in enumerate(TILES):
            if a < t0 + n and t0 < a + w:
                r.append(g_xT[k][ti])
        return r

    def mlp(l, wplan, tail=None):
        P.barrier()
        rmsnorm(GMLP + 8 * l)
        uT = arena[:, :].rearrange("p (c t) -> p c t", c=16)
        g_u = [[G() for _ in range(3)] for _ in range(16)]
        rl = [tmpA[:, 512 * i:512 * (i + 1)] for i in range(2)]
        g_rl = [G(), G()]
        rst = {"i": 0}
        idx = 0
        for half in range(2):
            tl = [0, 1] if half == 0 else [2, 3, 4]
            base = MT[tl[0]][0]
            if half == 1 and tail is not None:
                state["nb"], state["bankA"] = 4, 0
                P.record()
            for ffh in range(2):
                ups, downs = wplan[idx]
                idx += 1
                for j, wi in enumerate(ups):
                    wv, g_w = w_use(wi)
                    for m in range(4):
                        for ti in tl:
                            t0, n = MT[ti]
                            b = nextA()
                            for k in range(8):
                                mm(psA[:, b, 0:n], wv[:, k, m * 128:(m + 1) * 128], hT[:, k, t0:t0 + n], k == 0, k == 7,
                                   [g_w] + hT_gr(k, t0, n), [g_psA[b]])
                            i = rst["i"] % 2
                            rst["i"] += 1
                            c = j * 4 + m
                            act(rl[i][:, 0:n], psA[:, b, 0:n], AF.Relu, [g_psA[b]], [g_rl[i]])
                            tt(uT[:, c, t0 - base:t0 - base + n], rl[i][:, 0:n], rl[i][:, 0:n], ALU.mult,
                               [g_rl[i]], [g_u[c][tl.index(ti)]])
                for cq, wi in enumerate(downs):
                    wv, g_w = w_use(wi)
                    for ti in tl:
                        t0, n = MT[ti]
                        for m in range(2):
                            b = nextA()
                            for k in range(16):
                                mm(psA[:, b, 0:n], wv[:, k, m * 128:(m + 1) * 128],
                                   uT[:, k, t0 - base:t0 - base + n], k == 0, k == 15,
                                   [g_w, g_u[k][tl.index(ti)]], [g_psA[b]])
                            kx = cq * 2 + m
                            tt(xT[:, kx, t0:t0 + n], xT[:, kx, t0:t0 + n], psA[:, b, 0:n], ALU.add,
                               [g_psA[b]] + xg(kx, t0, n), xg(kx, t0, n))
            if half == 1 and tail is not None:
                l_b = P.stop()
                state["nb"], state["bankA"], state["base"] = 2, 0, 4
                P.record()
                tail()
                l_t = P.stop()
                state["nb"], state["bankA"], state["base"] = 6, 0, 0
                P.replay(l_b, l_t)

    if fl["mlp0"]:
        mlp(0, plan["mlp0"])

    g_kvs = G()
    relsb = sb("relsb", [32, 16], F32)
    rden = sb("rden", [128, 2, 4], F32)
    sbias = sb("sbias", [128, 16], F32)
    selb = sb("selb", [16, 2, 128], BF16)

    def kv_attn():
        P.barrier()
        kT2 = tmpA[:, 4096:8192].bitcast(BF16).rearrange("p (g t) -> p g t", g=4)
        vatt = tmpA[:, 8192:10304].bitcast(BF16).rearrange("p (j g d) -> p j g d", j=16, g=4)
        g_bias, g_rel, g_tab, g_vones, g_es = G(), G(), G(), G(), G()
        g_kT2 = [[G() for _ in range(4)] for _ in range(4)]
        g_kT2d = [G() for _ in range(4)]
        g_vatt = [G() for _ in range(16)]
        stage = nrm_r[0:32, :, :].rearrange("p a b -> p (a b)")
        dma(stage, t5d, (), [g_nr[0], g_nr[1]])
        dma(relsb[:], rel_bias, (), [g_rel])
        relh = arena[0:32, 0:16]
        rell = arena[0:32, 16:32]
        relt = arena[0:32, 32:64].bitcast(F32)
        ohb = arena[0:32, 64:576]
        g_rl2 = G()
        cp(relh, relsb[:], [g_rel], [g_rl2], eng="dve")
        tt(relt, relsb[:], relh, ALU.subtract, [g_rel, g_rl2], [g_rl2])
        cp(rell, relt, [g_rl2], [g_rl2], eng="dve")
        cp(ohb, stage[:, 0:512], [g_nr[0], g_nr[1]], [g_rl2], eng="dve")
        b = nextA()
        mm(psA[0:16, b, :], relh, ohb, True, False, [g_rl2], [g_psA[b]])
        mm(psA[0:16, b, :], rell, ohb, False, True, [g_rl2], [g_psA[b]])
        tt(kvs[0:16, :], psA[0:16, b, :], stage[0:16, 512:1024], ALU.add, [g_psA[b], g_nr[0], g_nr[1]], [g_kvs])
        if "tab" not in SKIP:
            dma(tabD, kvs[0:16, :], [g_kvs], [g_tab])
        biasR = tmpA[:, 4096:8192].rearrange("p (h b q) -> p h b q", h=16, b=2)
        g_biasR = G()
        for kb in range(2):
            src = bass.AP(tabD.tensor, kb * 256, [[1, 128], [512, 16], [1, 128]])
            if "toep" not in SKIP:
                dma(biasR[:, :, kb, :], src, [g_tab], [g_biasR])
        bRh = arena[:, 1024:1024 + 4096]
        bRl = arena[:, 5120:5120 + 4096]
        bRt = tmpA[:, 0:4096]
        Jb = arena[:, 9216:9216 + 128]
        g_bRh, g_bRl, g_bRt, g_Jb = G(), G(), G(), G()
        cp(Jb, cf[:, 770:898], [g_cf], [g_Jb], eng="act")
        cp(bRh, tmpA[:, 4096:8192], [g_biasR], [g_bRh], eng="act")
        tt(bRt, tmpA[:, 4096:8192], bRh, ALU.subtract, [g_biasR, g_bRh], [g_bRt])
        cp(bRl, bRt, [g_bRt], [g_bRl], eng="act")
        biasT = tmpA[:, 0:4096].rearrange("p (h b q) -> p h b q", h=16, b=2)
        for c in range(8):
            b = nextA()
            mm(psA[:, b, :], Jb, bRh[:, c * 512:(c + 1) * 512], True, False, [g_Jb, g_bRh], [g_psA[b]])
            mm(psA[:, b, :], Jb, bRl[:, c * 512:(c + 1) * 512], False, True, [g_Jb, g_bRl], [g_psA[b]])
            cp(tmpA[:, c * 512:(c + 1) * 512], psA[:, b, :], [g_psA[b]], [g_bias, g_bRt], eng=("act" if c % 2 else "dve"))
        cp(sbias[:], biasT[:, :, 1, 127], [g_bias], [g_bias], eng="dve")
        if "esink" not in SKIP:
            dma(esink[:], b_sink.broadcast_to([128, 16]), (), [g_es])
        act(esink[:], esink[:], AF.Exp, [g_es], [g_es])
        P.barrier()

        if "afterT5" in SKIP:
            return
        rmsnorm(GKV)
        wv, g_w = w_use(plan["kv"])
        for m in range(2):
            for ti in range(4):
                t0, n = TILES[ti]
                b = nextA()
                for k in range(8):
                    mm(psA[:, b, :], wv[:, k, m * 128:(m + 1) * 128], hT[:, k, t0:t0 + 512], k == 0, k == 7,
                       [g_w] + hT_gr(k, t0, n), [g_psA[b]])
                cp(kT2[0:64, 2 * m, t0:t0 + 512], psA[0:64, b, :], [g_psA[b]], [g_kT2[2 * m][ti]], eng="act")
                cp(kT2[64:128, 2 * m + 1, t0:t0 + 512], psA[64:128, b, :], [g_psA[b]], [g_kT2[2 * m + 1][ti]], eng="act")
        for g in range(4):
            if "dup" in SKIP:
                break
            if g % 2 == 0:
                dma(kT2[64:128, g, :], kT2[0:64, g, :], g_kT2[g], [g_kT2d[g]])
            else:
                dma(kT2[0:64, g, :], kT2[64:128, g, :], g_kT2[g], [g_kT2d[g]])
        if "afterK" in SKIP:
            return
        if "vones" not in SKIP:
            act(vatt[:, :, :, 64], onesb[:, 0:64].rearrange("p (j g) -> p j g", j=16), AF.Copy, [g_misc], [g_vones])
        for j in range(16):
            if "vmm" in SKIP:
                break
            if j == 15 and "vj15" in SKIP:
                break
            b = nextA()
            if j < 15:
                for k in range(8):
                    mm(psA[:, b, 0:256], hT[:, k, j * 128:(j + 1) * 128], wv[:, k, 256:512], k == 0, k == 7,
                       [g_w, g_hT[k][j]], [g_psA[b]])
                cp(vatt[:, j, :, 0:64], psA[:, b, 0:256].rearrange("p (g d) -> p g d", g=4), [g_psA[b]], [g_vatt[j]], eng="act")
            else:
                for k in range(8):
                    mm(psA[:, b, :], hT[:, k, j * 128:(j + 1) * 128], wv[:, k, :], k == 0, k == 7,
                       [g_w, g_hT[k][j]], [g_psA[b]])
                cp(nrm_r[:, 0, :], psA[:, b, :], [g_psA[b]], [g_nr[0]], eng="dve")
                cp(vatt[:, j, :, 0:64], nrm_r[:, 0, 256:512].rearrange("p (g d) -> p g d", g=4), [g_nr[0]], [g_vatt[j]], eng="act")
                dma(k_p, nrm_r[:, 0, 0:256], [g_nr[0]], [g_nr[0]], is_out=True)
                dma(v_p, nrm_r[:, 0, 256:512], [g_nr[0]], [g_nr[0]], is_out=True)
        if "afterV" in SKIP:
            return
        b = nextA()
        for k in range(8):
            mm(psA[0:16, b, :], hT[:, k, SEQ:T], wv[:, k, :], k == 0, k == 7, [g_w, g_hT[k][16]], [g_psA[b]])
        cp(kvs[0:16, :], psA[0:16, b, :], [g_psA[b]], [g_kvs], eng="dve")
        dma(k_s[:, 127, :], kvs[0:16, 0:256], [g_kvs], [], is_out=True)
        dma(v_s[:, 127, :], kvs[0:16, 256:512], [g_kvs], [], is_out=True)
        if "d2d" not in SKIP:
            dma(k_s[:, 0:127, :], ck[:, 1:128, :], (), [], is_out=True)
            dma(v_s[:, 0:127, :], cv[:, 1:128, :], (), [], is_out=True)
        if not fl["attn"]:
            return

        rmsnorm(GMIX + 8)
        qT = arena[:, 0:8 * T].rearrange("p (k t) -> p k t", k=8)
        g_q = [[G() for _ in range(5)] for _ in range(8)]
        for si, wi in enumerate(plan["q"]):
            wv, g_w = w_use(wi)

            def cons(m, ti, b, si=si):
                t0, n = TILES[ti]
                j = si * 4 + m
                act(qT[:, j, t0:t0 + n], psA[:, b, 0:n], AF.Copy, [g_psA[b]], [g_q[j][ti]], scale=ATTN_SCALE)
            proj_fm(wv, g_w, range(4), range(5), cons)

        pTb = [nrm_sq[:, 0, :], nrm_sq[:, 1, :]] + [nrm_r[:, i, :].bitcast(BF16)[:, j * 512:(j + 1) * 512]
                                                      for i in range(2) for j in range(2)]
        g_pT = [G() for _ in range(6)]
        pT_first = {2: g_nr[0], 3: g_nr[0], 4: g_nr[1], 5: g_nr[1], 0: g_nsq[0], 1: g_nsq[1]}
        g_atok = [G() for _ in range(4)]
        g_rden = [G(), G()]
        pT_of = {}

        def att1(n, g):
            ti = n // 4
            q0 = n * 128
            kbs = [1] if n == 0 else [0, 1]
            c0 = 256 if n == 0 else 0
            banks = [nextA(), nextA()]
            pTs = []
            nk = len(kbs)
            for hh in range(2):
                b = banks[hh]
                for jj in range(2):
                    j = 2 * g + jj
                    for kb in kbs:
                        kblk = n - 1 + kb
                        kg = g_kT2[g][kblk // 4] if hh == g % 2 else g_kT2d[g]
                        col = (kb * 2 + jj) * 128
                        mm(psA[:, b, col:col + 128], kT2[hh * 64:(hh + 1) * 64, g, kblk * 128:(kblk + 1) * 128],
                           qT[hh * 64:(hh + 1) * 64, j, q0:q0 + 128], True, True, [kg, g_q[j][ti]], [g_psA[b]])
            for hh in range(2):
                b = banks[hh]
                bv = biasT[:, 4 * g + hh:4 * g + hh + 3:2, :, :].transpose([0, 2, 1, 3])
                if n == 0:
                    bv = bv[:, 1:2]
                pv4 = psA[:, b, c0:512].rearrange("p (a j q) -> p a j q", a=nk, j=2)
                tt(pv4, pv4, bv, ALU.add, [g_psA[b], g_bias], [g_psA[b]])
                pi = (2 * (n * 4 + g) + hh) % 6
                wl = [g_pT[pi]]
                if pi in pT_first:
                    wl.append(pT_first.pop(pi))
                act(pTb[pi][:, c0:512], psA[:, b, c0:512], AF.Exp, [g_psA[b]], wl)
                pTs.append((pTb[pi], g_pT[pi]))
            pT_of[(n, g)] = pTs

        def att2(n, g):
            q0 = n * 128
            kbs = [1] if n == 0 else [0, 1]
            pTs = pT_of.pop((n, g))
            bo = nextA()
            for hi in range(4):
                hh, jj = hi % 2, hi // 2
                for kb in kbs:
                    kblk = n - 1 + kb
                    col = (kb * 2 + jj) * 128
                    mm(psA[:, bo, hi * 65:(hi + 1) * 65], pTs[hh][0][:, col:col + 128], vatt[:, kblk, g, 0:65],
                       kb == kbs[0], kb == 1, [pTs[hh][1], g_vatt[kblk], g_vones], [g_psA[bo]])
            ov = psA[:, bo, 0:260].rearrange("p (h d) -> p h d", h=4)
            ri = (n * 4 + g) % 2
            tt(rden[:, ri, :], ov[:, :, 64], esink[:, 4 * g:4 * g + 4], ALU.add, [g_psA[bo], g_es], [g_rden[ri]])
            P.add("dve", lambda e, ri=ri: e.reciprocal(rden[:, ri, :], rden[:, ri, :]), [g_rden[ri]], [g_rden[ri]])
            tt(atok[:, g * 256:(g + 1) * 256].rearrange("p (h d) -> p h d", h=4), ov[:, :, 0:64],
               rden[:, ri, :].unsqueeze(2).broadcast_to([128, 4, 64]), ALU.mult,
               [g_psA[bo], g_rden[ri]], [g_atok[g]])
            if g == 3:
                bB = nextB()
                for j in range(8):
                    tr(psB[:, bB, j * 128:(j + 1) * 128], atok[:, j * 128:(j + 1) * 128], identb[:],
                       [g_atok[j // 2], g_misc], [g_psB[bB]])
                cp(hT[:, :, q0:q0 + 128], psB[:, bB, :].rearrange("p (k t) -> p k t", k=8), [g_psB[bB]],
                   [g_hT[k][n] for k in range(8)], eng="act")

        ngs = [(n, g) for n in range(16) for g in range(4)]
        att1(*ngs[0])
        att1(*ngs[1])
        for i, ng in enumerate(ngs):
            if i + 2 < len(ngs):
                att1(*ngs[i + 2])
            att2(*ng)

        phase_barrier()
        j0_, j1_ = plan["bout"]
        bw = [w_use(j0_, limit=j1_), w_use(j1_, limit=j1_)]
        state["nb"] = 3
        state["bankA"] = 0
        P.record()
        for si in range(2):
            proj_fm(bw[si][0], bw[si][1], range(4), range(4), add_into_x(si * 4))
        l_proj = P.stop()
        P.record()
        Knew = tmpA[:, 0:4096].rearrange("p (s c) -> p s c", s=16)
        Vnew = tmpA[:, 4096:8192].rearrange("p (s c) -> p s c", s=16)
        Vnb = tmpA[:, 8192:10240].bitcast(BF16).rearrange("p (s c) -> p s c", s=16)
        g_Kn, g_Vn, g_Vnb = G(), G(), G()
        dma(Knew[0:127, :, :], ck[:, 1:128, :].rearrange("s r c -> r s c"), [g_phase], [g_Kn])
        dma(Knew[127:128, :, :], kvs[0:16, 0:256], [g_kvs, g_phase], [g_Kn])
        dma(Vnew[0:127, :, :], cv[:, 1:128, :].rearrange("s r c -> r s c"), [g_phase], [g_Vn])
        dma(Vnew[127:128, :, :], kvs[0:16, 256:512], [g_kvs, g_phase], [g_Vn])
        cp(Vnb, Vnew, [g_Vn], [g_Vnb], eng="act")
        qtok = arena[0:16, T:T + 1024]
        g_qtok = G()
        bB = nextB()
        for j in range(8):
            tr(psB[0:16, bB, j * 128:(j + 1) * 128], qT[:, j, SEQ:T], identb[:], [g_q[j][4], g_misc], [g_psB[bB]])
        cp(qtok, psB[0:16, bB, :], [g_psB[bB]], [g_qtok], eng="dve")
        prod = arena[:, 0:2048].bitcast(F32)
        g_prod = G()
        sS = arena[:, 2 * T:2 * T + 512].bitcast(F32)
        pS = arena[:, 3 * T:3 * T + 512].bitcast(F32)
        pSb = arena[:, 4 * T:4 * T + 256]
        g_sS, g_pS, g_pSb, g_sel = G(), G(), G(), [G(), G()]
        for s_ in range(16):
            si = s_ % 2
            cp(selb[:, si, :], identb[0:16, s_:s_ + 1].broadcast_to([16, 128]), [g_misc], [g_sel[si]], eng="dve")
            b0 = 3
            for hf in range(2):
                mm(psA[:, b0 + hf, :], selb[:, si, :], qtok[:, hf * 512:(hf + 1) * 512], True, True,
                   [g_sel[si], g_qtok], [g_psA[b0 + hf]])
            tt(prod.rearrange("p (g j d) -> p g j d", g=4, j=4),
               Knew[:, s_, :].rearrange("p (g d) -> p g d", g=4).unsqueeze(2).broadcast_to([128, 4, 4, 64]),
               psA[:, b0:b0 + 2, :].rearrange("p b (j d) -> p (b j) d", d=64).rearrange("p (g j) d -> p g j d", g=4),
               ALU.mult, [g_Kn, g_psA[b0], g_psA[b0 + 1]], [g_prod])
            P.add("dve", lambda e, s_=s_: e.tensor_reduce(sS[:, s_ * 16:(s_ + 1) * 16],
                                                           prod.rearrange("p (h d) -> p h d", d=64),
                                                           mybir.AxisListType.X, ALU.add), [g_prod], [g_sS])
        tt(sS.rearrange("p (s h) -> p s h", s=16), sS.rearrange("p (s h) -> p s h", s=16),
           sbias[:].unsqueeze(1).broadcast_to([128, 16, 16]), ALU.add, [g_sS, g_bias], [g_sS])
        act(pS, sS, AF.Exp, [g_sS], [g_pS])
        cp(pSb, pS, [g_pS], [g_pSb], eng="dve")
        b = 5
        mm(psA[:, b, 0:256], onesb[:], pSb, True, True, [g_pSb, g_misc], [g_psA[b]])
        tt(sS.rearrange("p (s h) -> p s h", s=16), psA[:, b, 0:256].rearrange("p (s h) -> p s h", s=16),
           esink[:].unsqueeze(1).broadcast_to([128, 16, 16]), ALU.add, [g_psA[b], g_es, g_sS], [g_sS])
        P.add("dve", lambda e: e.reciprocal(sS, sS), [g_sS], [g_sS])
        tt(pSb, pS, sS, ALU.mult, [g_pS, g_sS], [g_pSb])
        b = 5
        pv = pSb.rearrange("p (s h) -> p s h", s=16)
        for s_ in range(16):
            for g in range(4):
                for hh in range(2):
                    c = (2 * g) * 16 + s_
                    mm(psA[hh * 64:(hh + 1) * 64, b, c:c + 17:16], Vnb[:, s_, g * 64:(g + 1) * 64],
                       pv[:, s_, 4 * g + hh:4 * g + hh + 3:2], True, True, [g_Vnb, g_pSb], [g_psA[b]])
        cp(hT[:, :, SEQ:T], psA[:, b, 0:128].rearrange("p (k t) -> p k t", k=8), [g_psA[b]],
           [g_hT[k][16] for k in range(8)], eng="act")

        if dbg is not None and fl.get("dbg") == "aT":
            dma(dbg[:, 0:2048].bitcast(BF16).rearrange("p (k t) -> p k t", k=8), hT[:, :, 0:512],
                hT_all(0, 512), [], is_out=True)
        l_smp = P.stop()
        nd = 0
        while nd < len(l_smp) and l_smp[nd][4]:
            nd += 1
        k_ = (len(l_proj) * 3) // 5
        P.replay(l_smp[:nd])
        P.replay(l_proj[:k_])
        P.replay(l_proj[k_:], l_smp[nd:])
        state["nb"] = 6
        for si in range(2):
            proj_fm(bw[si][0], bw[si][1], range(4), [4], add_into_x(si * 4))

    if fl["kv"]:
        kv_attn()

    FO = 1024
    zst = [tmpA[:, FO + 1024 * i:FO + 1024 * (i + 1)].rearrange("p (k t) -> p k t", k=8) for i in range(2)]
    g_zst = [[G() for _ in range(8)] for _ in range(2)]
    yst = [tmpA[:, FO + 2048 + 1024 * i:FO + 2048 + 1024 * (i + 1)] for i in range(2)]
    g_yst = [G(), G()]
    zs = tmpA[:, FO + 4096:FO + 4096 + 128].rearrange("p (k t) -> p k t", k=8)
    g_zs = G()
    yss = tmpA[0:16, FO + 4224:FO + 4224 + 1024]
    g_yss = G()

    def final_norm(tiles):
        for ti in tiles:
            t0, n = TILES[ti]
            b = nextA()
            for k in range(8):
                i = k % 2
                act(nrm_sq[:, i, 0:n], xT[:, k, t0:t0 + n], AF.Square, [g_xT[k][ti]], [g_nsq[i]])
                mm(psA[:, b, 0:n], onesb[:], nrm_sq[:, i, 0:n], k == 0, k == 7, [g_nsq[i], g_misc], [g_psA[b]])
            r, g_r = rstd_from_ps(psA[:, b, 0:n], n, 1.0 / 1024.0, g_psA[b])
            if ti == 4:
                for k in range(8):
                    stt(zs[:, k, :], xT[:, k, t0:t0 + n], vcol[:, GFIN + k:GFIN + k + 1], r, ALU.mult, ALU.mult,
                        [g_xT[k][ti], g_r, g_vcol], [g_zs])
                b0 = nextA()
                b1 = nextA()
                for kk_ in range(8):
                    bb = b0 if kk_ < 4 else b1
                    tr(psA[0:16, bb, (kk_ % 4) * 128:(kk_ % 4 + 1) * 128], zs[:, kk_, :], ident,
                       [g_zs, g_cf], [g_psA[bb]])
                cp(yss[:, 0:512], psA[0:16, b0, :], [g_psA[b0]], [g_yss], eng="act")
                cp(yss[:, 512:1024], psA[0:16, b1, :], [g_psA[b1]], [g_yss], eng="act")
                dma(y_s, yss, [g_yss], [], is_out=True)
                continue
            for sub in range(4):
                j = ti * 4 + sub
                s = j % 2
                for k in range(8):
                    stt(zst[s][:, k, :], xT[:, k, t0 + sub * 128:t0 + (sub + 1) * 128],
                        vcol[:, GFIN + k:GFIN + k + 1], r[:, sub * 128:(sub + 1) * 128], ALU.mult, ALU.mult,
                        [g_xT[k][ti], g_r, g_vcol], [g_zst[s][k]])
                b0 = nextA()
                b1 = nextA()
                for kk_ in range(8):
                    bb = b0 if kk_ < 4 else b1
                    tr(psA[:, bb, (kk_ % 4) * 128:(kk_ % 4 + 1) * 128], zst[s][:, kk_, :], ident,
                       [g_zst[s][kk_], g_cf], [g_psA[bb]])
                cp(yst[s][:, 0:512], psA[:, b0, :], [g_psA[b0]], [g_yst[s]], eng="act")
                cp(yst[s][:, 512:1024], psA[:, b1, :], [g_psA[b1]], [g_yst[s]], eng="act")
                dma(y_p[j * 128:(j + 1) * 128, :], yst[s], [g_yst[s]], [g_yst[s]], is_out=True)

    if fl["mlp1"]:
        mlp(1, plan["mlp1"], tail=lambda: final_norm([0, 1]))
        final_norm([2, 3, 4])
    else:
        P.barrier()
        final_norm(range(5))

    with nc.Block() as block:
        P.emit(nc, block, ES)
    ES.close()
    return nc


_CACHE = {}


def kernel(x_prompt, x_sample, state_hgrn, cache_k_win, cache_v_win, w_a_in, a_lb, a_gnorm, w_a_out,
           g_mix, g_mlp, g_kv, w_kv, w_b_q, b_sink, w_b_out, rel_bias, w_up, w_down, g_final, _flags=None):
    f32 = lambda a: np.ascontiguousarray(np.asarray(a, dtype=np.float32))
    key = repr(sorted((_flags or {}).items()))
    if key not in _CACHE:
        _CACHE[key] = build(_flags)
    nc = _CACHE[key]
    cfc, t5c = make_consts()
    shared = {
        "w_a_in": f32(w_a_in)[0], "a_lb": f32(a_lb), "a_gnorm": f32(a_gnorm), "w_a_out": f32(w_a_out)[0],
        "g_mix": f32(g_mix), "g_mlp": f32(g_mlp), "g_kv": f32(g_kv).reshape(1, 1024), "w_kv": f32(w_kv),
        "w_b_q": f32(w_b_q)[0], "b_sink": f32(b_sink), "w_b_out": f32(w_b_out)[0], "rel_bias": f32(rel_bias),
        "w_up": f32(w_up), "w_down": f32(w_down), "g_final": f32(g_final).reshape(1, 1024),
        "cf_d": cfc, "t5_d": t5c,
    }
    xpn, xsn = f32(x_prompt), f32(x_sample)
    stn, ckn, cvn = f32(state_hgrn), f32(cache_k_win), f32(cache_v_win)
    in_maps = []
    for c in range(NCORES):
        m = dict(shared)
        m["xp"] = xpn[c]
        m["xs"] = xsn[c * NS:(c + 1) * NS, 0, :]
        m["st"] = stn[0, c * NS:(c + 1) * NS]
        m["ck"] = ckn[c * NS:(c + 1) * NS].reshape(NS, 128, 256)
        m["cv"] = cvn[c * NS:(c + 1) * NS].reshape(NS, 128, 256)
        in_maps.append(m)
    if _flags and _flags.get("trace"):
        res = run_bass_kernel_spmd(nc, in_maps, core_ids=list(range(NCORES)), trace=True)
        print("EXEC_TIME_NS", res.exec_time_ns)
    else:
        res = run_bass_kernel_spmd(nc, in_maps, core_ids=list(range(NCORES)))
    R = res.results
    y_prompt = np.stack([R[c]["y_p"] for c in range(NCORES)], 0)
    y_sample = np.concatenate([R[c]["y_s"] for c in range(NCORES)], 0).reshape(128, 1, 1024)
    s_prompt = np.stack([R[c]["st_p"] for c in range(NCORES)], 0)[None]
    s_sample = np.concatenate([R[c]["st_s"] for c in range(NCORES)], 0)[None]
    k_prompt = np.stack([R[c]["k_p"] for c in range(NCORES)], 0).reshape(8, 128, 4, 64)
    v_prompt = np.stack([R[c]["v_p"] for c in range(NCORES)], 0).reshape(8, 128, 4, 64)
    k_sample = np.concatenate([R[c]["k_s"] for c in range(NCORES)], 0).reshape(128, 128, 4, 64)
    v_sample = np.concatenate([R[c]["v_s"] for c in range(NCORES)], 0).reshape(128, 128, 4, 64)
    if _flags and _flags.get("dbg"):
        global DBG
        DBG = [R[c]["dbg"] for c in range(NCORES)]
    return (y_prompt, y_sample, s_prompt, s_sample, k_prompt, v_prompt, k_sample, v_sample)
```
